# Optimizing a Trainium2 kernel written in Bass

```python
import math
import jax, jax.numpy as jnp
from jax import lax
import numpy as np

D_MODEL = 1024
BATCH = 32
SEQ = 2048
DEPTH = 2

ROPE_THETA = 500000.0
ROPE_FRACTION = 4
NORM_EPS = 1e-6
Q_BLOCK = 128

H_A = 4
DH_A = 64
H_B = 4
DK_B = 128
DV_B = 128
CONV_B = 4
CHUNK_B = 64
H_C = 8
Q_LORA_C = 256
KV_LORA_C = 128
NOPE_C = 64
ROPE_C = 32
V_C = 64
H_D = 8
DH_D = 64
DIL_GROUPS = ((128, 1), (512, 4), (2048, 16))
N_DIL = 3
D_BLOCK = 128

N_BRANCH = 4
BRANCH_W = 512
DEEPNORM_ALPHA = (2.0 * DEPTH) ** 0.25
DEEPNORM_BETA = (8.0 * DEPTH) ** -0.25

IN_SPLITS = (
    H_A * 2 * DH_A, H_A * 2 * DH_A, H_A * 2 * DH_A,
    H_B * DK_B, H_B * DK_B, H_B * DV_B, H_B, H_B,
    Q_LORA_C, KV_LORA_C, ROPE_C,
    3 * N_DIL * H_D * DH_D,
    N_BRANCH * BRANCH_W,
    N_BRANCH * D_MODEL,
)
D_IN = sum(IN_SPLITS)

kernel_name = 'hybrid_gated_diff_delta_mla_dilated'


def _layernorm(x):
    xf = x.astype(jnp.float32)
    mu = jnp.mean(xf, -1, keepdims=True)
    var = jnp.mean(jnp.square(xf - mu), -1, keepdims=True)
    return ((xf - mu) * lax.rsqrt(var + NORM_EPS)).astype(x.dtype)


def _rmsnorm(x, g):
    xf = x.astype(jnp.float32)
    y = xf * lax.rsqrt(jnp.mean(jnp.square(xf), -1, keepdims=True) + NORM_EPS)
    return (y * g.astype(jnp.float32)).astype(x.dtype)


def _l2norm(x):
    xf = x.astype(jnp.float32)
    return (xf * lax.rsqrt(jnp.sum(jnp.square(xf), -1, keepdims=True) + NORM_EPS)).astype(x.dtype)


def _rope(x, pos, rot_dim):
    half = rot_dim // 2
    inv_freq = ROPE_THETA ** (-jnp.arange(half, dtype=jnp.float32) / half)
    ang = pos.astype(jnp.float32)[..., None] * inv_freq
    cos = jnp.cos(ang)[:, :, None, :]
    sin = jnp.sin(ang)[:, :, None, :]
    xr = x[..., :rot_dim].astype(jnp.float32)
    x1, x2 = xr[..., :half], xr[..., half:]
    rot = jnp.concatenate([x1 * cos - x2 * sin, x2 * cos + x1 * sin], -1).astype(x.dtype)
    return jnp.concatenate([rot, x[..., rot_dim:]], -1)


def _causal_block_sweep(attend, seq):
    return jnp.concatenate([attend(s, s + Q_BLOCK) for s in range(0, seq, Q_BLOCK)], axis=1)


def _causal_mask(s, e):
    return jnp.arange(s, e)[:, None] >= jnp.arange(e)[None, :]


def _diff_attention(q, k, v, lam, pos):
    B, S = q.shape[:2]
    rot = DH_A // ROPE_FRACTION
    q = _rope(q.reshape(B, S, H_A * 2, DH_A), pos, rot).reshape(B, S, H_A, 2, DH_A)
    k = _rope(k.reshape(B, S, H_A * 2, DH_A), pos, rot).reshape(B, S, H_A, 2, DH_A)
    scale = DH_A ** -0.5

    def attend(s, e):
        sc = jnp.einsum('bqhmd,bkhmd->bhmqk', q[:, s:e], k[:, :e]).astype(jnp.float32) * scale
        sc = jnp.where(_causal_mask(s, e), sc, -jnp.inf)
        p = jax.nn.softmax(sc, axis=-1)
        p = p[:, :, 0] - lam * p[:, :, 1]
        return jnp.einsum('bhqk,bkhd->bqhd', p.astype(v.dtype), v[:, :e])

    return _causal_block_sweep(attend, S)


def _short_conv(x, w):
    S = x.shape[1]
    xp = jnp.pad(x, ((0, 0), (CONV_B - 1, 0), (0, 0)))
    return sum(xp[:, j:j + S] * w[j] for j in range(CONV_B))


def _gated_delta_rule(q, k, v, g, beta):
    B, S, H, dk = q.shape
    dv = v.shape[-1]
    C = CHUNK_B
    N = S // C
    f32 = jnp.float32

    def chunks(t):
        return t.astype(f32).reshape(B, N, C, H, -1).transpose(0, 3, 1, 2, 4)

    qc = chunks(q) * (dk ** -0.5)
    kc = chunks(k)
    vc = chunks(v)
    bc = beta.astype(f32).reshape(B, N, C, H).transpose(0, 3, 1, 2)
    gc = jnp.cumsum(g.astype(f32).reshape(B, N, C, H).transpose(0, 3, 1, 2), axis=-1)
    tril = jnp.tril(jnp.ones((C, C), bool))
    strict = jnp.tril(jnp.ones((C, C), bool), -1)
    diff = gc[..., :, None] - gc[..., None, :]
    decay = jnp.where(tril, jnp.exp(jnp.where(tril, diff, 0.0)), 0.0)
    k_beta = kc * bc[..., None]
    lower = jnp.where(strict, jnp.einsum('bhncd,bhnmd->bhncm', k_beta, kc) * decay, 0.0)
    a_mat = jnp.eye(C, dtype=f32) + lower
    rhs = jnp.concatenate([vc * bc[..., None], k_beta * jnp.exp(gc)[..., None]], -1)
    sol = lax.linalg.triangular_solve(a_mat, rhs, left_side=True, lower=True, unit_diagonal=True)
    u, w = sol[..., :dv], sol[..., dv:]
    qk = jnp.where(tril, jnp.einsum('bhncd,bhnmd->bhncm', qc, kc) * decay, 0.0)

    def step(state, xs):
        q_i, k_i, u_i, w_i, qk_i, g_i = xs
        v_new = u_i - jnp.einsum('bhcd,bhde->bhce', w_i, state)
        o = (jnp.einsum('bhcd,bhde->bhce', q_i * jnp.exp(g_i)[..., None], state)
             + jnp.einsum('bhcm,bhme->bhce', qk_i, v_new))
        g_last = g_i[..., -1]
        state = (state * jnp.exp(g_last)[..., None, None]
                 + jnp.einsum('bhcd,bhce->bhde', k_i * jnp.exp(g_last[..., None] - g_i)[..., None], v_new))
        return state, o

    xs = tuple(jnp.moveaxis(t, 2, 0) for t in (qc, kc, u, w, qk, gc))
    state0 = jnp.zeros((B, H, dk, dv), f32)
    _, o = lax.scan(step, state0, xs)
    return o.transpose(1, 0, 3, 2, 4).reshape(B, S, H, dv).astype(v.dtype)


def _mla(c_q, c_kv, k_pe, q_norm_c, w_uq, kv_norm_c, w_ukv, pos):
    B, S = c_q.shape[:2]
    q = (_rmsnorm(c_q, q_norm_c) @ w_uq).reshape(B, S, H_C, NOPE_C + ROPE_C)
    kv = (_rmsnorm(c_kv, kv_norm_c) @ w_ukv).reshape(B, S, H_C, NOPE_C + V_C)
    q_nope, q_pe = q[..., :NOPE_C], _rope(q[..., NOPE_C:], pos, ROPE_C)
    k_nope, v = kv[..., :NOPE_C], kv[..., NOPE_C:]
    k_pe = _rope(k_pe[:, :, None, :], pos, ROPE_C)[:, :, 0]
    scale = (NOPE_C + ROPE_C) ** -0.5

    def attend(s, e):
        sc = (jnp.einsum('bqhd,bkhd->bhqk', q_nope[:, s:e], k_nope[:, :e])
              + jnp.einsum('bqhd,bkd->bhqk', q_pe[:, s:e], k_pe[:, :e])).astype(jnp.float32) * scale
        sc = jnp.where(_causal_mask(s, e), sc, -jnp.inf)
        p = jax.nn.softmax(sc, axis=-1)
        return jnp.einsum('bhqk,bkhd->bqhd', p.astype(v.dtype), v[:, :e])

    return _causal_block_sweep(attend, S)


def _dilated_group(q, k, v, dilation, steps):
    B, S, H, dh = q.shape
    L = S // dilation
    nb = -(-L // D_BLOCK)
    Lp = nb * D_BLOCK

    def sub(t):
        t = t.reshape(B, L, dilation, H, dh).transpose(0, 2, 1, 3, 4)
        t = jnp.pad(t, ((0, 0), (0, 0), (0, Lp - L), (0, 0), (0, 0)))
        return t.reshape(B, dilation, nb, D_BLOCK, H, dh)

    def with_prev(t):
        prev = jnp.pad(t, ((0, 0), (0, 0), (1, 0), (0, 0), (0, 0), (0, 0)))[:, :, :-1]
        return jnp.concatenate([prev, t], axis=3)

    qs, kk, vv = sub(q), with_prev(sub(k)), with_prev(sub(v))
    a = jnp.arange(D_BLOCK)[:, None]
    b = jnp.arange(2 * D_BLOCK)[None, :]
    rel = D_BLOCK + a - b
    blk = jnp.arange(nb)[:, None, None]
    valid = (rel >= 0) & (rel <= steps) & ((blk > 0) | (b >= D_BLOCK))
    sc = jnp.einsum('bdnqhe,bdnkhe->bdnhqk', qs, kk).astype(jnp.float32) * (dh ** -0.5)
    sc = jnp.where(valid[:, None], sc, -jnp.inf)
    lse = jax.nn.logsumexp(sc, axis=-1)
    p = jnp.exp(sc - lse[..., None])
    o = jnp.einsum('bdnhqk,bdnkhe->bdnqhe', p.astype(vv.dtype), vv)

    def unsub(t):
        t = t.reshape(B, dilation, Lp, *t.shape[4:])[:, :, :L]
        return t.swapaxes(1, 2).reshape(B, S, *t.shape[3:])

    return unsub(o), unsub(jnp.swapaxes(lse, -1, -2)[..., None])[..., 0]


def _dilated_attention(q, k, v, pos):
    rot = DH_D // ROPE_FRACTION
    outs, lses = [], []
    for gi, (window, dil) in enumerate(DIL_GROUPS):
        o, l = _dilated_group(_rope(q[:, :, gi], pos, rot), _rope(k[:, :, gi], pos, rot),
                              v[:, :, gi], dil, window // dil)
        outs.append(o)
        lses.append(l)
    wts = jax.nn.softmax(jnp.stack(lses, -1), axis=-1)
    return jnp.einsum('bshg,gbshe->bshe', wts.astype(outs[0].dtype), jnp.stack(outs, 0))


def _mixer_sublayer(h, pos, layer, w_in, conv_b, a_log, dt_bias, out_norm_b,
                    lambda_q1, lambda_k1, lambda_q2, lambda_k2, subln_g,
                    q_norm_c, w_uq, kv_norm_c, w_ukv, w_br, w_out):
    B, S, _ = h.shape
    f32 = jnp.float32
    offsets = [int(o) for o in np.cumsum(IN_SPLITS)[:-1]]
    (q_a, k_a, v_a, q_b, k_b, v_b, beta_b, decay_b, cq_c, ckv_c, kpe_c,
     qkv_d, z, merge) = jnp.split(h @ w_in, offsets, axis=-1)

    lam_init = 0.8 - 0.6 * math.exp(-0.3 * layer)
    lam = (jnp.exp(jnp.sum(lambda_q1.astype(f32) * lambda_k1.astype(f32)))
           - jnp.exp(jnp.sum(lambda_q2.astype(f32) * lambda_k2.astype(f32))) + lam_init)
    o_a = _diff_attention(q_a.reshape(B, S, H_A, 2, DH_A), k_a.reshape(B, S, H_A, 2, DH_A),
                          v_a.reshape(B, S, H_A, 2 * DH_A), lam, pos)
    o_a = (_rmsnorm(o_a, subln_g) * (1.0 - lam_init)).reshape(B, S, BRANCH_W)

    qkv_b = jax.nn.silu(_short_conv(jnp.concatenate([q_b, k_b, v_b], -1), conv_b))
    q_b, k_b, v_b = jnp.split(qkv_b, [H_B * DK_B, 2 * H_B * DK_B], axis=-1)
    beta = jax.nn.sigmoid(beta_b.astype(f32))
    g = -jnp.exp(a_log.astype(f32)) * jax.nn.softplus(decay_b.astype(f32) + dt_bias.astype(f32))
    o_b = _gated_delta_rule(_l2norm(q_b.reshape(B, S, H_B, DK_B)), _l2norm(k_b.reshape(B, S, H_B, DK_B)),
                            v_b.reshape(B, S, H_B, DV_B), g, beta)
    o_b = _rmsnorm(o_b, out_norm_b).reshape(B, S, BRANCH_W)

    o_c = _mla(cq_c, ckv_c, kpe_c, q_norm_c, w_uq, kv_norm_c, w_ukv, pos).reshape(B, S, BRANCH_W)

    qkv_d = qkv_d.reshape(B, S, 3, N_DIL, H_D, DH_D)
    o_d = _dilated_attention(qkv_d[:, :, 0], qkv_d[:, :, 1], qkv_d[:, :, 2], pos).reshape(B, S, BRANCH_W)

    branches = jnp.stack([o_a, o_b, o_c, o_d], axis=2) * jax.nn.silu(z).reshape(B, S, N_BRANCH, BRANCH_W)
    gates = jax.nn.sigmoid(merge).reshape(B, S, N_BRANCH, D_MODEL)
    merged = jnp.einsum('bsnc,ncd,bsnd->bsd', branches, w_br, gates)
    return merged @ w_out


def setup_inputs(seed: int = 0) -> dict:
    key = jax.random.key(seed)
    ks = jax.random.split(key, 24)
    f32 = jnp.float32
    L = DEPTH

    def nrm(k, shape, scale):
        return jax.random.normal(k, shape, f32) * scale

    def gain(k, shape):
        return 1.0 + 0.02 * jax.random.normal(k, shape, f32)

    x = nrm(ks[0], (BATCH, SEQ, D_MODEL), 1.0)
    c = nrm(ks[1], (BATCH, D_MODEL), 1.0)
    offs = jax.random.randint(ks[2], (BATCH, 1), 0, 4096, dtype=jnp.int32)
    positions = offs + jnp.arange(SEQ, dtype=jnp.int32)[None, :]
    w_ada = nrm(ks[3], (L, D_MODEL, 3 * D_MODEL), D_MODEL ** -0.5)
    b_ada = nrm(ks[4], (L, 3 * D_MODEL), 0.02)
    w_in = nrm(ks[5], (L, D_MODEL, D_IN), D_MODEL ** -0.5)
    conv_b = nrm(ks[6], (L, CONV_B, 2 * H_B * DK_B + H_B * DV_B), CONV_B ** -0.5)
    a_log = jnp.log(jax.random.uniform(ks[7], (L, H_B), f32, 1.0, 16.0))
    dt = jnp.exp(jax.random.uniform(ks[8], (L, H_B), f32, math.log(1e-3), math.log(1e-1)))
    dt_bias = dt + jnp.log(-jnp.expm1(-dt))
    out_norm_b = gain(ks[9], (L, DV_B))
    lambda_q1 = nrm(ks[10], (L, DH_A), 0.1)
    lambda_k1 = nrm(ks[11], (L, DH_A), 0.1)
    lambda_q2 = nrm(ks[12], (L, DH_A), 0.1)
    lambda_k2 = nrm(ks[13], (L, DH_A), 0.1)
    subln_g = gain(ks[14], (L, 2 * DH_A))
    q_norm_c = gain(ks[15], (L, Q_LORA_C))
    w_uq = nrm(ks[16], (L, Q_LORA_C, H_C * (NOPE_C + ROPE_C)), Q_LORA_C ** -0.5)
    kv_norm_c = gain(ks[17], (L, KV_LORA_C))
    w_ukv = nrm(ks[18], (L, KV_LORA_C, H_C * (NOPE_C + V_C)), KV_LORA_C ** -0.5)
    w_br = nrm(ks[19], (L, N_BRANCH, BRANCH_W, D_MODEL), DEEPNORM_BETA * BRANCH_W ** -0.5)
    w_out = nrm(ks[20], (L, D_MODEL, D_MODEL), DEEPNORM_BETA * D_MODEL ** -0.5)
    ln_g = gain(ks[21], (L, D_MODEL))
    ln_b = nrm(ks[22], (L, D_MODEL), 0.02)
    return {'x': x, 'c': c, 'positions': positions, 'w_ada': w_ada, 'b_ada': b_ada,
            'w_in': w_in, 'conv_b': conv_b, 'a_log': a_log, 'dt_bias': dt_bias,
            'out_norm_b': out_norm_b, 'lambda_q1': lambda_q1, 'lambda_k1': lambda_k1,
            'lambda_q2': lambda_q2, 'lambda_k2': lambda_k2, 'subln_g': subln_g,
            'q_norm_c': q_norm_c, 'w_uq': w_uq, 'kv_norm_c': kv_norm_c, 'w_ukv': w_ukv,
            'w_br': w_br, 'w_out': w_out, 'ln_g': ln_g, 'ln_b': ln_b}


def reference(x, c, positions, w_ada, b_ada, w_in, conv_b, a_log, dt_bias, out_norm_b,
              lambda_q1, lambda_k1, lambda_q2, lambda_k2, subln_g, q_norm_c, w_uq,
              kv_norm_c, w_ukv, w_br, w_out, ln_g, ln_b):
    c_act = jax.nn.silu(c)
    for l in range(DEPTH):
        shift, scale, gate = jnp.split(c_act @ w_ada[l] + b_ada[l], 3, axis=-1)
        h = _layernorm(x) * (1.0 + scale[:, None, :]) + shift[:, None, :]
        y = _mixer_sublayer(h, positions, l, w_in[l], conv_b[l], a_log[l], dt_bias[l], out_norm_b[l],
                            lambda_q1[l], lambda_k1[l], lambda_q2[l], lambda_k2[l], subln_g[l],
                            q_norm_c[l], w_uq[l], kv_norm_c[l], w_ukv[l], w_br[l], w_out[l])
        x = _layernorm(DEEPNORM_ALPHA * x + gate[:, None, :] * y) * ln_g[l] + ln_b[l]
    return x
```

```python
import contextlib
import numpy as np
import concourse.bass as bass
import concourse.mybir as mybir
from concourse.bass_utils import run_bass_kernel_spmd

F32 = mybir.dt.float32
BF16 = mybir.dt.bfloat16
I32 = mybir.dt.int32
AF = mybir.ActivationFunctionType
ALU = mybir.AluOpType
AX = mybir.AxisListType

ENGS = ("pe", "act", "dve", "pool", "sp")
SEM_LIMIT = 30000
N_DMA_SEMS = 20


class T:
    __slots__ = ("name", "w", "r", "excl")

    def __init__(self, name="", excl=False):
        self.name = name
        self.w = None
        self.r = []
        self.excl = excl


class Op:
    __slots__ = ("eng", "fn", "deps", "needed", "sem", "val", "is_dma", "idx", "waits", "slot")

    def __init__(self, eng, fn, is_dma):
        self.eng = eng
        self.fn = fn
        self.deps = []
        self.needed = False
        self.sem = None
        self.val = None
        self.is_dma = is_dma
        self.idx = None
        self.waits = None
        self.slot = None


class Sched:
    def __init__(self, nc):
        self.nc = nc
        self.ops = {e: [] for e in ENGS}
        self.all_dma = []
        self.stack = contextlib.ExitStack()
        self.dma_last = [None] * N_DMA_SEMS

    def sbuf(self, name, shape, dtype):
        return self.stack.enter_context(self.nc.sbuf_tensor(name, list(shape), dtype))

    def psum(self, name, shape, dtype=F32):
        return self.stack.enter_context(self.nc.psum_tensor(name, list(shape), dtype))

    def _sem(self, name):
        return self.stack.enter_context(self.nc.semaphore(name))

    def op(self, eng, fn, reads=(), writes=(), dma=False):
        o = Op(eng, fn, dma)
        o.idx = len(self.ops[eng])
        deps = []
        for t in reads:
            if t.w is not None:
                deps.append(t.w)
            if t.excl:
                deps.extend(r for r in t.r if r.eng != eng)
        for t in writes:
            if t.w is not None:
                deps.append(t.w)
            deps.extend(t.r)
        if dma:
            slot = len(self.all_dma) % N_DMA_SEMS
            self.all_dma.append(o)
            o.slot = slot
            if self.dma_last[slot] is not None:
                deps.append(self.dma_last[slot])
            self.dma_last[slot] = o
        o.deps = deps
        for t in reads:
            t.r.append(o)
        for t in writes:
            t.w = o
            t.r = []
        self.ops[eng].append(o)
        return o

    def dma(self, eng, out, in_, reads=(), writes=(), **kw):
        return self.op(eng, lambda e: e.dma_start(out=out, in_=in_, **kw), reads, writes, dma=True)

    def barrier(self):
        last = [self.ops[e][-1] for e in ENGS if self.ops[e]]
        last = [o for o in last if o.fn is not None]
        dm = [o for o in self.dma_last if o is not None]
        for e in ENGS:
            o = Op(e, None, False)
            o.idx = len(self.ops[e])
            o.deps = [d for d in last if d.eng != e or d.is_dma] + dm
            for d in last:
                if d.eng == e and not d.is_dma:
                    o.deps.append(d)
            self.ops[e].append(o)

    def finish(self, tiles):
        self.op("sp", None, reads=tiles)

    def _prep(self):
        for e in ENGS:
            seen_eng = {}
            seen_dma = set()
            for o in self.ops[e]:
                best = {}
                dmas = []
                for d in o.deps:
                    if d is o:
                        continue
                    if d.is_dma:
                        if id(d) not in seen_dma:
                            seen_dma.add(id(d))
                            dmas.append(d)
                    else:
                        if d.fn is None:
                            continue
                        if d.eng == "pe" and e == "pe" and not o.is_dma and o.fn is not None:
                            continue
                        if seen_eng.get(d.eng, -1) >= d.idx:
                            continue
                        if d.eng not in best or best[d.eng].idx < d.idx:
                            best[d.eng] = d
                w = []
                for src, d in best.items():
                    seen_eng[src] = d.idx
                    d.needed = True
                    w.append(d)
                w.extend(dmas)
                o.waits = w
        n_slots = min(N_DMA_SEMS, max(1, len(self.all_dma)))
        dma_sems = [self._sem("dq%d" % i) for i in range(n_slots)]
        dma_val = [0] * N_DMA_SEMS
        for o in self.all_dma:
            dma_val[o.slot] += 16
            o.sem = dma_sems[o.slot]
            o.val = dma_val[o.slot]
        self.n_sems = n_slots
        for e in ENGS:
            cur = None
            v = 0
            n_ep = 0
            for o in self.ops[e]:
                if (not o.is_dma) and o.needed:
                    if cur is None or v >= SEM_LIMIT:
                        cur = self._sem("e_%s%d" % (e, n_ep))
                        n_ep += 1
                        self.n_sems += 1
                        v = 0
                    v += 1
                    o.sem = cur
                    o.val = v

    def run(self):
        nc = self.nc
        self._prep()
        with nc.Block() as block:
            def replay(e):
                def f(engine):
                    for o in self.ops[e]:
                        for d in o.waits:
                            engine.wait_ge(d.sem, d.val)
                        if o.fn is None:
                            continue
                        ins = o.fn(engine)
                        if o.is_dma:
                            ins.then_inc(o.sem, 16)
                        elif o.needed:
                            ins.then_inc(o.sem, 1)
                return f
            block.tensor(replay("pe"))
            block.scalar(replay("act"))
            block.vector(replay("dve"))
            block.gpsimd(replay("pool"))
            block.sync(replay("sp"))
        self.stack.close()

    def stats(self):
        return {e: len(self.ops[e]) for e in ENGS}


S_ = 2048
D_ = 1024
NT = 16
KC = 8
DEPTH = 2
D_IN = 14248
ALPHA = (2.0 * DEPTH) ** 0.25
EPS = 1e-6
THETA = 500000.0
OFF_QA, OFF_KA, OFF_VA = 0, 512, 1024
OFF_QB, OFF_KB, OFF_VB, OFF_BD = 1536, 2048, 2560, 3072
OFF_C = 3080
OFF_D = 3496
OFF_Z = 8104
OFF_MG = 10152
NEG = -30000.0
C_ID, C_LE, C_LT, C_NLE, C_ONE, C_F8, C_F16, C_B8, C_W8, C_W16, C_W32, C_W64, C_NGE, C_END = 0, 128, 256, 384, 512, 640, 648, 664, 792, 920, 1048, 1176, 1304, 1432
TWO_PI = float(2.0 * np.pi)
PI = float(np.pi)


def make_consts():
    c = np.zeros((128, C_END), np.float32)
    p = np.arange(128)[:, None]
    f = np.arange(128)[None, :]
    c[:, C_ID:C_ID + 128] = (p == f)
    c[:, C_LE:C_LE + 128] = (p <= f)
    c[:, C_LT:C_LT + 128] = (p < f)
    c[:, C_NLE:C_NLE + 128] = np.where(p <= f, 0.0, NEG)
    c[:, C_NGE:C_NGE + 128] = np.where(p >= f, 0.0, NEG)
    c[:, C_ONE:C_ONE + 128] = 1.0
    c[:, C_F8:C_F8 + 8] = (THETA ** (-np.arange(8, dtype=np.float32) / np.float32(8))).astype(np.float32)[None, :]
    c[:, C_F16:C_F16 + 16] = (THETA ** (-np.arange(16, dtype=np.float32) / np.float32(16))).astype(np.float32)[None, :]
    c[:, C_B8:C_B8 + 128] = (p // 8 == f // 8)
    for s_, off in ((8, C_W8), (16, C_W16), (32, C_W32), (64, C_W64)):
        c[:, off:off + 128] = (p // (2 * s_) == f // (2 * s_)) & (p // s_ != f // s_)
    return c


class Builder:
    def __init__(self, nseq=4, layers=(0, 1), mixers="ABCD", dump=False, final=True):
        self.nseq = nseq
        self.layers = tuple(layers)
        self.mixers = mixers
        self.dump = dump
        self.final = final
        self.nc = bass.Bass("TRN2", target_bir_lowering=False)
        self.s = Sched(self.nc)
        self.declare_io()
        self.alloc()
        self.build()

    def declare_io(self):
        nc = self.nc
        B = self.nseq

        def din(name, shape, dt=F32):
            return nc.dram_tensor(name, list(shape), dt, kind="ExternalInput").ap()
        self.x = din("x", [B, S_, D_])
        self.pos = din("pos_tm", [B, 128, NT], I32)
        self.cT = din("cT", [128, KC, B])
        self.w_ada = din("w_ada", [DEPTH, D_, 3 * D_])
        self.b_ada = din("b_ada", [DEPTH, 3 * D_])
        self.b_ada_fm = din("b_ada_fm", [DEPTH, 128, 24])
        self.w_in = din("w_in", [DEPTH, D_, D_IN])
        self.w_d = din("w_d", [DEPTH, D_, 4608])
        self.conv_fm = din("conv_fm", [DEPTH, 128, 12, 4])
        self.a_log = din("a_log", [DEPTH, 4])
        self.dt_bias = din("dt_bias", [DEPTH, 4])
        self.out_norm_b = din("out_norm_b", [DEPTH, 128])
        self.lq1 = din("lambda_q1", [DEPTH, 64])
        self.lk1 = din("lambda_k1", [DEPTH, 64])
        self.lq2 = din("lambda_q2", [DEPTH, 64])
        self.lk2 = din("lambda_k2", [DEPTH, 64])
        self.subln_g = din("subln_g", [DEPTH, 128])
        self.qn_fm = din("qn_fm", [DEPTH, 128, 2])
        self.kvn_fm = din("kvn_fm", [DEPTH, 128, 1])
        self.w_uq = din("w_uq", [DEPTH, 256, 768])
        self.w_ukv = din("w_ukv", [DEPTH, 128, 1024])
        self.w_br = din("w_br", [DEPTH, 4, 512, 1024])
        self.w_out = din("w_out", [DEPTH, D_, D_])
        self.ln_g = din("ln_g", [DEPTH, D_])
        self.ln_b = din("ln_b", [DEPTH, D_])
        self.consts_d = din("consts", [128, C_END])
        self.y = nc.dram_tensor("y", [B, S_, D_], F32, kind="ExternalOutput").ap()
        self.xs = nc.dram_tensor("xs", [B, S_, D_], F32).ap()
        self.gate_d = nc.dram_tensor("gate_d", [DEPTH, 4, D_], F32).ap()
        self.Tgd = T()
        self.Txs = [[T() for _ in range(NT)] for _ in range(B)]
        self.Ty = [[T() for _ in range(NT)] for _ in range(B)]
        if self.dump:
            self.o_dump = nc.dram_tensor("o_dump", [S_, 2048], BF16, kind="ExternalOutput").ap()
            self.h_dump = nc.dram_tensor("h_dump", [D_, S_], BF16, kind="ExternalOutput").ap()
            self.Tdump = T()

    def alloc(self):
        s = self.s
        self.cst = s.sbuf("cst", [128, C_END], F32); self.Tcst = T()
        self.identb = s.sbuf("identb", [128, 128], BF16)
        self.onesb = s.sbuf("onesb", [128, 128], BF16)
        self.m_le4 = s.sbuf("m_le4", [128, 512], BF16)
        self.m_lg2 = s.sbuf("m_lg2", [128, 512], BF16)
        self.Tcb = T()
        self.epsc = s.sbuf("epsc", [128, 4], F32)
        self.hT = s.sbuf("hT", [128, KC, S_], BF16)
        self.ThT = [T() for _ in range(NT)]
        self.o_all = s.sbuf("o_all", [128, NT, 2048], BF16)
        self.To = [[T() for _ in range(4)] for _ in range(NT)]
        self.NW = 3
        self.wbuf = [s.sbuf("wbuf%d" % i, [128, 4096], BF16) for i in range(self.NW)]
        self.Tw = [T() for _ in range(self.NW)]
        self.wi = 0
        self.ps = [s.psum("ps%d" % i, [128, 512], F32) for i in range(8)]
        self.Tps = [T(excl=True) for _ in range(8)]
        self.cos8 = s.sbuf("cos8", [128, NT, 8], F32)
        self.sin8 = s.sbuf("sin8", [128, NT, 8], F32)
        self.cos16 = s.sbuf("cos16", [128, NT, 16], F32)
        self.sin16 = s.sbuf("sin16", [128, NT, 16], F32)
        self.Trope = T()
        self.lng = s.sbuf("lng", [128, D_], F32)
        self.lnb = s.sbuf("lnb", [128, D_], F32)
        self.gate_bc = s.sbuf("gate_bc", [128, D_], F32)
        self.Tlnp = T(); self.Tgate = T()
        self.sublng = s.sbuf("sublng", [128, 128], F32)
        self.onormb = s.sbuf("onormb", [128, 128], F32)
        self.convw = s.sbuf("convw", [128, 12, 4], F32)
        self.qnw = s.sbuf("qnw", [128, 2], F32)
        self.kvnw = s.sbuf("kvnw", [128, 1], F32)
        self.alog_bc = s.sbuf("alog_bc", [128, 4], F32)
        self.dtb_bc = s.sbuf("dtb_bc", [128, 4], F32)
        self.Tlp = T()
        self.lam = s.sbuf("lam", [128, 2 * DEPTH], F32)
        self.Tlam = T()
        self.shiftT = s.sbuf("shiftT", [128, DEPTH, KC, 4], F32)
        self.scale1T = s.sbuf("scale1T", [128, DEPTH, KC, 4], F32)
        self.Tada = T()
        self.ARENA = 64000
        self.arena = s.sbuf("arena", [128, self.ARENA // 2], BF16)
        self.a_off = 0

    def a_reset(self):
        self.a_off = 0

    def a_get(self, shape, dt):
        n = int(np.prod(shape))
        nb = n * (2 if dt == BF16 else 4)
        nb = (nb + 63) // 64 * 64
        assert self.a_off + nb <= self.ARENA, (self.a_off, nb)
        ap = self.arena[:, self.a_off // 2:(self.a_off + nb) // 2]
        self.a_off += nb
        if dt != BF16:
            ap = ap.bitcast(dt)
        ap = ap[:, 0:n]
        if len(shape) == 2:
            return ap.rearrange("p (a b) -> p a b", b=shape[1])
        if len(shape) == 3:
            return ap.rearrange("p (a b c) -> p a b c", b=shape[1], c=shape[2])
        return ap

    def mm(self, out, lhsT, rhs, start, stop, R, W):
        self.s.op("pe", lambda e: e.matmul(out, lhsT=lhsT, rhs=rhs, start=start, stop=stop), R, W)

    def tr(self, out, in_, ident, R, W):
        self.s.op("pe", lambda e: e.transpose(out=out, in_=in_, identity=ident), R, W)

    def act(self, out, in_, func, R, W, bias=None, scale=1.0, accum=None):
        kw = {}
        if bias is not None:
            kw["bias"] = bias
        if accum is not None:
            kw["accum_out"] = accum
        self.s.op("act", lambda e: e.activation(out=out, in_=in_, func=func, scale=scale, **kw), R, W)

    def tt(self, eng, out, in0, in1, op, R, W):
        self.s.op(eng, lambda e: e.tensor_tensor(out=out, in0=in0, in1=in1, op=op), R, W)

    def ts(self, eng, out, in0, s1, op0, R, W, s2=None, op1=None):
        if op1 is None:
            self.s.op(eng, lambda e: e.tensor_scalar(out=out, in0=in0, scalar1=s1, scalar2=None, op0=op0), R, W)
        else:
            self.s.op(eng, lambda e: e.tensor_scalar(out=out, in0=in0, scalar1=s1, scalar2=s2, op0=op0, op1=op1), R, W)

    def stt(self, eng, out, in0, sc, in1, op0, op1, R, W):
        self.s.op(eng, lambda e: e.scalar_tensor_tensor(out=out, in0=in0, scalar=sc, in1=in1, op0=op0, op1=op1), R, W)

    def tss(self, eng, out, in_, sc, op, R, W):
        self.s.op(eng, lambda e: e.tensor_single_scalar(out=out, in_=in_, scalar=sc, op=op), R, W)

    def cp(self, eng, out, in_, R, W):
        if eng == "act":
            self.s.op("act", lambda e: e.copy(out=out, in_=in_), R, W)
        else:
            self.s.op(eng, lambda e: e.tensor_copy(out=out, in_=in_), R, W)

    def red(self, eng, out, in_, op, R, W):
        self.s.op(eng, lambda e: e.tensor_reduce(out=out, in_=in_, axis=AX.X, op=op), R, W)

    def recip(self, out, in_, R, W):
        self.s.op("dve", lambda e: e.reciprocal(out=out, in_=in_), R, W)

    def memset(self, eng, ap, v, W):
        self.s.op(eng, lambda e: e.memset(ap, v), (), W)

    def rsqrt(self, out, in_, scale, R, W, eps_mul=1.0):
        if eps_mul == 1.0:
            self.act(out, in_, AF.Sqrt, R + [self.Tcb], W, bias=self.epsc[:, 0:1], scale=scale)
        else:
            self.act(out, in_, AF.Sqrt, R + [self.Tcb], W, bias=self.epsc[:, 1:2], scale=scale)
        self.recip(out, out, W, W)

    def wload(self, src_ap, view):
        i = self.wi
        self.wi = (self.wi + 1) % self.NW
        buf = self.wbuf[i]
        n = int(np.prod(view))
        dst = buf[:, 0:n]
        if len(view) == 2:
            dst = dst.rearrange("p (a b) -> p a b", b=view[1])
        self.s.dma("pool", dst, src_ap, writes=[self.Tw[i]])
        return dst, self.Tw[i]

    def build(self):
        s = self.s
        self.phase0()
        if self.mixers != "ABCD":
            allT = [t for row in self.To for t in row]
            self.memset("pool", self.o_all[:], 0.0, allT)
        for b in range(self.nseq):
            self.seq_setup(b)
            for l in self.layers:
                self.layer_setup(b, l)
                s.barrier()
                self.phase_ln(b, l)
                s.barrier()
                if "A" in self.mixers:
                    self.mixer_a(b, l)
                    s.barrier()
                if "C" in self.mixers:
                    self.mixer_c(b, l)
                    s.barrier()
                if "D" in self.mixers:
                    self.mixer_d(b, l)
                    s.barrier()
                if "B" in self.mixers:
                    self.mixer_b(b, l)
                    s.barrier()
                if self.dump:
                    self.do_dump()
                    s.barrier()
                if self.final:
                    self.phase_final(b, l)
                    s.barrier()
        last = self.layers[-1]
        fin = []
        for b in range(self.nseq):
            if self.final:
                fin += self.Ty[b]
        if self.dump:
            fin.append(self.Tdump)
        s.finish(fin)
        s.run()

    def do_dump(self):
        s = self.s
        allT = [t for row in self.To for t in row]
        s.dma("sp", self.o_dump.rearrange("(t p) c -> p t c", p=128), self.o_all[:], reads=allT, writes=[self.Tdump])
        s.dma("sp", self.h_dump.rearrange("(k p) t -> p k t", p=128), self.hT[:], reads=self.ThT, writes=[self.Tdump])

    def phase0(self):
        s = self.s
        cst = self.cst
        s.dma("sp", cst[:], self.consts_d, writes=[self.Tcst])
        R = [self.Tcst]; W = [self.Tcb]
        self.cp("dve", self.identb[:], cst[:, C_ID:C_ID + 128], R, W)
        self.cp("dve", self.onesb[:], cst[:, C_ONE:C_ONE + 128], R, W)
        for j in range(4):
            self.cp("dve", self.m_le4[:, j * 128:(j + 1) * 128], cst[:, C_NLE:C_NLE + 128], R, W)
            src = C_NLE if j % 2 == 0 else C_NGE
            self.cp("dve", self.m_lg2[:, j * 128:(j + 1) * 128], cst[:, src:src + 128], R, W)
        self.memset("pool", self.epsc[:, 0:1], EPS, W)
        self.memset("pool", self.epsc[:, 1:2], EPS * 128.0, W)
        self.memset("pool", self.epsc[:, 2:3], 1.0, W)
        self.memset("pool", self.epsc[:, 3:4], 0.0, W)
        self.a_reset()
        lt = self.a_get([4, 64], F32)
        tmpl = self.a_get([1, 64], F32)
        e2 = self.a_get([1, 4], F32)
        Tl = T()
        for l in range(DEPTH):
            for j, src in enumerate((self.lq1, self.lk1, self.lq2, self.lk2)):
                s.dma("sp", lt[:, j, :], src[l:l + 1, :].partition_broadcast(128), writes=[Tl])
            for m in range(2):
                self.tt("dve", tmpl[:, 0, :], lt[:, 2 * m, :], lt[:, 2 * m + 1, :], ALU.mult, [Tl], [Tl])
                self.red("dve", e2[:, 0, m:m + 1], tmpl[:, 0, :], ALU.add, [Tl], [Tl])
            self.act(e2[:, 0, 2:4], e2[:, 0, 0:2], AF.Exp, [Tl], [Tl])
            lam_init = 0.8 - 0.6 * float(np.exp(-0.3 * l))
            self.tt("dve", e2[:, 0, 0:1], e2[:, 0, 2:3], e2[:, 0, 3:4], ALU.subtract, [Tl], [Tl])
            self.ts("dve", self.lam[:, 2 * l:2 * l + 1], e2[:, 0, 0:1], lam_init, ALU.add, [Tl], [self.Tlam])
            self.ts("dve", self.lam[:, 2 * l + 1:2 * l + 2], self.lam[:, 2 * l:2 * l + 1], -1.0, ALU.mult, [self.Tlam], [self.Tlam])
        B = self.nseq
        cact = self.a_get([KC, B], F32)
        badaf = self.a_get([DEPTH, 24], F32)
        brow = self.a_get([DEPTH, 1024], F32)
        wa = [self.a_get([KC, 512], F32) for _ in range(2)]
        grow = self.a_get([DEPTH, 1024], F32)
        Tc = T(); Tb = T(); Twa = [T(), T()]; Tgr = T()
        s.dma("sp", cact, self.cT, writes=[Tc])
        self.act(cact, cact, AF.Silu, [Tc], [Tc])
        s.dma("sp", badaf, self.b_ada_fm.rearrange("l p c -> p l c"), writes=[Tb])
        for l in range(DEPTH):
            s.dma("sp", brow[0:1, l, :], self.b_ada[l:l + 1, 2048:3072], writes=[Tb])
        k = 0
        for l in range(DEPTH):
            for cb in range(6):
                w = wa[k % 2]; Tw_ = Twa[k % 2]; k += 1
                s.dma("sp", w, self.w_ada[l].rearrange("(kc p) c -> p kc c", p=128)[:, :, cb * 512:(cb + 1) * 512], writes=[Tw_])
                if cb < 4:
                    for j in range(4):
                        fc = cb * 4 + j
                        ps = self.ps[fc % 4]; Tp = self.Tps[fc % 4]
                        for kc in range(KC):
                            self.mm(ps[:, 0:B], w[:, kc, j * 128:(j + 1) * 128], cact[:, kc, :], kc == 0, kc == KC - 1, [Tw_, Tc], [Tp])
                        if fc < 8:
                            self.ts("dve", self.shiftT[:, l, fc, 0:B], ps[:, 0:B], badaf[:, l, fc:fc + 1], ALU.add, [Tp, Tb], [self.Tada])
                        else:
                            self.ts("dve", self.scale1T[:, l, fc - 8, 0:B], ps[:, 0:B], badaf[:, l, fc:fc + 1], ALU.add, [Tp, Tb], [self.Tada],
                                    s2=1.0, op1=ALU.add)
                else:
                    half = cb - 4
                    ps = self.ps[4 + half]; Tp = self.Tps[4 + half]
                    for kc in range(KC):
                        self.mm(ps[0:B, :], cact[:, kc, :], w[:, kc, :], kc == 0, False, [Tw_, Tc], [Tp])
                    self.mm(ps[0:B, :], cst[0:1, C_ONE:C_ONE + B], brow[0:1, l, half * 512:(half + 1) * 512], False, True, [self.Tcst, Tb], [Tp])
                    self.cp("dve", grow[0:B, l, half * 512:(half + 1) * 512], ps[0:B, :], [Tp], [Tgr])
                    if half == 1:
                        s.dma("sp", self.gate_d[l, 0:B, :], grow[0:B, l, :], reads=[Tgr], writes=[self.Tgd])
        s.barrier()

    def seq_setup(self, b):
        s = self.s
        s.barrier()
        self.a_reset()
        pi_ = self.a_get([1, NT], I32)
        pf = self.a_get([1, NT], F32)
        Tp = T()
        s.dma("sp", pi_[:, 0, :], self.pos[b], writes=[Tp])
        self.cp("dve", pf[:, 0, :], pi_[:, 0, :], [Tp], [Tp])
        for half, cosT, sinT, coff in ((8, self.cos8, self.sin8, C_F8), (16, self.cos16, self.sin16, C_F16)):
            ang = self.a_get([NT, half], F32)
            kf = self.a_get([NT, half], F32)
            ki = self.a_get([NT, half], I32)
            msk = self.a_get([NT, half], F32)
            Ta = T()
            R = [Ta, Tp, self.Tcst]
            self.tt("dve", ang, pf[:, 0, :].unsqueeze(2).broadcast_to([128, NT, half]),
                    self.cst[:, coff:coff + half].unsqueeze(1).broadcast_to([128, NT, half]), ALU.mult, R, [Ta])

            def reduce_(t):
                self.ts("dve", kf, t, 1.0 / TWO_PI, ALU.mult, [Ta], [Ta])
                self.cp("dve", ki, kf, [Ta], [Ta])
                self.cp("dve", kf, ki, [Ta], [Ta])
                self.stt("dve", t, kf, -TWO_PI, t, ALU.mult, ALU.add, [Ta], [Ta])
                self.tss("dve", msk, t, PI, ALU.is_gt, [Ta], [Ta])
                self.stt("dve", t, msk, -TWO_PI, t, ALU.mult, ALU.add, [Ta], [Ta])
                self.tss("dve", msk, t, -PI, ALU.is_lt, [Ta], [Ta])
                self.stt("dve", t, msk, TWO_PI, t, ALU.mult, ALU.add, [Ta], [Ta])
            reduce_(ang)
            self.act(sinT[:], ang, AF.Sin, [Ta], [self.Trope])
            self.ts("dve", ang, ang, PI / 2.0, ALU.add, [Ta], [Ta])
            reduce_(ang)
            self.act(cosT[:], ang, AF.Sin, [Ta], [self.Trope])

    def layer_setup(self, b, l):
        s = self.s
        s.barrier()
        s.dma("sp", self.lng[:], self.ln_g[l:l + 1, :].partition_broadcast(128), writes=[self.Tlnp])
        s.dma("sp", self.lnb[:], self.ln_b[l:l + 1, :].partition_broadcast(128), writes=[self.Tlnp])
        W = [self.Tlp]
        s.dma("sp", self.sublng[:], self.subln_g[l:l + 1, :].partition_broadcast(128), writes=W)
        s.dma("sp", self.onormb[:], self.out_norm_b[l:l + 1, :].partition_broadcast(128), writes=W)
        s.dma("sp", self.convw[:], self.conv_fm[l], writes=W)
        s.dma("sp", self.qnw[:], self.qn_fm[l], writes=W)
        s.dma("sp", self.kvnw[:], self.kvn_fm[l], writes=W)
        s.dma("sp", self.alog_bc[:], self.a_log[l:l + 1, :].partition_broadcast(128), writes=W)
        s.dma("sp", self.dtb_bc[:], self.dt_bias[l:l + 1, :].partition_broadcast(128), writes=W)
        s.dma("sp", self.gate_bc[:], self.gate_d[l, b:b + 1, :].partition_broadcast(128), reads=[self.Tgd], writes=[self.Tgate])
        lam_init = 0.8 - 0.6 * float(np.exp(-0.3 * l))
        self.ts("dve", self.sublng[:], self.sublng[:], 1.0 - lam_init, ALU.mult, W, W)

    def xsrc(self, b, l):
        if l == self.layers[0]:
            return self.x[b], None
        return self.xs[b], self.Txs[b]

    def phase_ln(self, b, l):
        s = self.s
        self.a_reset()
        NB = 3
        xt = [self.a_get([1, D_], F32) for _ in range(NB)]
        xn = [self.a_get([1, D_], BF16) for _ in range(NB)]
        st = [self.a_get([1, 12], F32) for _ in range(NB)]
        mv = [self.a_get([1, 4], F32) for _ in range(NB)]
        Tx = [T() for _ in range(NB)]; Tn = [T() for _ in range(NB)]; Tm = [T() for _ in range(NB)]
        src, Tsrc = self.xsrc(b, l)
        for tt in range(NT):
            i = tt % NB
            R = [] if Tsrc is None else [Tsrc[tt]]
            s.dma("sp", xt[i][:, 0, :], src[tt * 128:(tt + 1) * 128, :], reads=R, writes=[Tx[i]])
            for hh in range(2):
                s.op("dve", (lambda o_, i_: (lambda e: e.bn_stats(out=o_, in_=i_)))(st[i][:, 0, hh * 6:(hh + 1) * 6], xt[i][:, 0, hh * 512:(hh + 1) * 512]), [Tx[i]], [Tm[i]])
            s.op("dve", (lambda o_, i_: (lambda e: e.bn_aggr(out=o_, in_=i_)))(mv[i][:, 0, 0:2], st[i][:, 0, :]), [Tm[i]], [Tm[i]])
            self.rsqrt(mv[i][:, 0, 2:3], mv[i][:, 0, 1:2], 1.0, [Tm[i]], [Tm[i]])
            self.stt("dve", mv[i][:, 0, 3:4], mv[i][:, 0, 0:1], -1.0, mv[i][:, 0, 2:3], ALU.mult, ALU.mult, [Tm[i]], [Tm[i]])
            self.ts("dve", xn[i][:, 0, :], xt[i][:, 0, :], mv[i][:, 0, 2:3], ALU.mult, [Tx[i], Tm[i]], [Tn[i]], s2=mv[i][:, 0, 3:4], op1=ALU.add)
            pb = self.ps[tt % 2]; Tp = self.Tps[tt % 2]
            pv = pb[:].bitcast(BF16)
            for kc in range(KC):
                self.tr(pv[:, kc * 128:(kc + 1) * 128], xn[i][:, 0, kc * 128:(kc + 1) * 128], self.identb[:], [Tn[i], self.Tcb], [Tp])
            for kc in list(range(0, KC, 2)) + list(range(1, KC, 2)):
                if kc % 2 == 0:
                    self.act(self.hT[:, kc, tt * 128:(tt + 1) * 128], pv[:, kc * 128:(kc + 1) * 128], AF.Identity, [Tp, self.Tada], [self.ThT[tt]],
                             bias=self.shiftT[:, l, kc, b:b + 1], scale=self.scale1T[:, l, kc, b:b + 1])
                else:
                    self.ts("dve", self.hT[:, kc, tt * 128:(tt + 1) * 128], pv[:, kc * 128:(kc + 1) * 128], self.scale1T[:, l, kc, b:b + 1], ALU.mult,
                            [Tp, self.Tada], [self.ThT[tt]], s2=self.shiftT[:, l, kc, b:b + 1], op1=ALU.add)

    def proj_tm(self, tt, wv, Tw_, n, ps, Tp):
        for kc in range(KC):
            self.mm(ps, self.hT[:, kc, tt * 128:(tt + 1) * 128], wv[:, kc, :], kc == 0, kc == KC - 1, [self.ThT[tt], Tw_], [Tp])

    def win_cols(self, l, c0, n):
        return self.w_in[l].rearrange("(kc p) c -> p kc c", p=128)[:, :, c0:c0 + n]

    def rope_tm(self, tt, x3, o3, nm, half, R, W, tmp):
        cosT, sinT = (self.cos8, self.sin8) if half == 8 else (self.cos16, self.sin16)
        cb = cosT[:, tt, :].unsqueeze(1).broadcast_to([128, nm, half])
        sb = sinT[:, tt, :].unsqueeze(1).broadcast_to([128, nm, half])
        x1 = x3[:, :, 0:half]; x2 = x3[:, :, half:2 * half]
        t1 = tmp[0][:, 0:nm, 0:half]; t2 = tmp[1][:, 0:nm, 0:half]
        Tt = tmp[2]
        RR = R + [self.Trope]
        import os
        dbg = os.environ.get("DBG_A", "")
        if "ropeA" in dbg:
            return
        self.tt("dve", t1, x1, cb, ALU.mult, RR, [Tt])
        self.tt("dve", t2, x2, sb, ALU.mult, RR, [Tt])
        if "ropeB" in dbg:
            return
        self.tt("dve", o3[:, :, 0:half], t1, t2, ALU.subtract, [Tt], W)
        if "ropeC" in dbg:
            return
        self.tt("dve", t1, x2, cb, ALU.mult, RR, [Tt])
        self.tt("dve", t2, x1, sb, ALU.mult, RR, [Tt])
        self.tt("dve", o3[:, :, half:2 * half], t1, t2, ALU.add, [Tt], W)

    def causal_attn(self, nheads, qT, kT, vfn, dv, scale, Tq, Tk, Tv, evac, pT, TpT, groups=None):
        ST = [5, 6, 7]
        ACC = [1, 2, 3, 4]
        LA = 3
        NP = len(pT)
        if groups is None:
            groups = [(h, h) for h in range(nheads)]
        iters = []
        for (hk, h) in groups:
            for qc in range(4):
                for kt in range(4 * qc + 4):
                    iters.append((hk, h, qc, kt))
        accs = [(self.ps[ACC[j]][:, 0:dv + 1], self.Tps[ACC[j]]) for j in range(4)]

        def stage1(i):
            hk, h, qc, kt = iters[i]
            q_ = qT(h); k_ = kT(h)
            q0 = max(qc * 512, kt * 128)
            n = (qc + 1) * 512 - q0
            bank = ST[i % 3]; pi = i % NP
            sps = self.ps[bank]; Tsp = self.Tps[bank]
            diag = kt >= 4 * qc
            if diag and n > 128:
                self.mm(sps[:, 128:n], k_[:, kt * 128:(kt + 1) * 128], q_[:, q0 + 128:q0 + n], True, True, [Tq, Tk], [Tsp])
            if diag:
                self.mm(sps[:, 0:128], k_[:, kt * 128:(kt + 1) * 128], q_[:, q0:q0 + 128], True, False, [Tq, Tk], [Tsp])
                self.mm(sps[:, 0:128], self.identb[:], self.m_le4[:, 0:128], False, True, [self.Tcb], [Tsp])
            else:
                self.mm(sps[:, 0:n], k_[:, kt * 128:(kt + 1) * 128], q_[:, q0:q0 + n], True, True, [Tq, Tk], [Tsp])
            self.act(pT[pi][:, 0:n], sps[:, 0:n], AF.Exp, [Tsp], [TpT[pi]], scale=scale)

        def stage2(i):
            hk, h, qc, kt = iters[i]
            q0 = max(qc * 512, kt * 128)
            pi = i % NP
            for j in range(4):
                qt = qc * 4 + j
                if qt < kt:
                    continue
                c0 = qt * 128 - q0
                acc, Tacc = accs[j]
                self.mm(acc, pT[pi][:, c0:c0 + 128], vfn(h, kt), kt == 0, kt == qt, [TpT[pi], Tv], [Tacc])
                if kt == qt:
                    evac(hk, qt, acc, Tacc)
        n_it = len(iters)
        for i in range(n_it + LA):
            if i < n_it:
                stage1(i)
            if i - LA >= 0:
                stage2(i - LA)

    def mixer_a(self, b, l):
        s = self.s
        self.a_reset()
        qT = self.a_get([4, S_], BF16)
        kT = self.a_get([4, S_], BF16)
        V = self.a_get([NT, 4 * 129], BF16)
        Tq = T(); Tk = T(); Tv = T()
        V4 = V.rearrange("p t (h c) -> p t h c", c=129)
        self.memset("pool", V4[:, :, :, 128:129], 1.0, [Tv])
        NB = 2
        mark = self.a_off
        qk = [self.a_get([1, 512], BF16) for _ in range(NB)]
        Tqk = [T() for _ in range(NB)]
        tmp = [self.a_get([16, 16], F32), self.a_get([16, 16], F32), T()]
        xf = [self.a_get([1, 512], F32) for _ in range(NB)]; Txf = [T() for _ in range(NB)]
        wsrc = [self.win_cols(l, OFF_QA, 512), self.win_cols(l, OFF_KA, 512), self.win_cols(l, OFF_VA, 512)]
        import os
        dbg = os.environ.get("DBG_A", "")
        for wi_, ws in enumerate(wsrc):
            if "onlyv" in dbg and wi_ != 2:
                continue
            if "onlyq" in dbg and wi_ != 0:
                continue
            wv, Tw_ = self.wload(ws, [KC, 512])

            def first(tt, wi_=wi_, wv=wv, Tw_=Tw_):
                bank = (tt % 2)
                ps = self.ps[bank]; Tp = self.Tps[bank]
                self.proj_tm(tt, wv, Tw_, 512, ps[:], Tp)
                if wi_ == 2:
                    self.cp("act", V4[:, tt, :, 0:128], ps[:].rearrange("p (h c) -> p h c", c=128), [Tp], [Tv])
                else:
                    i = tt % NB
                    dst = qk[i][:, 0, 0:512]
                    o3 = dst.rearrange("p (m c) -> p m c", c=64)
                    self.cp("act", xf[i][:, 0, :], ps[:], [Tp], [Txf[i]])
                    xf3 = xf[i][:, 0, :].rearrange("p (m c) -> p m c", c=64)
                    self.cp("pool", o3[:, :, 16:64], xf3[:, :, 16:64], [Txf[i]], [Tqk[i]])
                    self.rope_tm(tt, xf3, o3, 8, 8, [Txf[i]], [Tqk[i]], tmp)

            def second(tt, wi_=wi_):
                if wi_ == 2:
                    return
                i = tt % NB
                dst = qk[i][:, 0, 0:512]
                tb = 2 + (tt % 2)
                pv = self.ps[tb][:].bitcast(BF16); Tt_ = self.Tps[tb]
                for c in range(4):
                    self.tr(pv[:, c * 128:(c + 1) * 128], dst[:, c * 128:(c + 1) * 128], self.identb[:], [Tqk[i], self.Tcb], [Tt_])
                dstT = qT if wi_ == 0 else kT
                self.cp("dve", dstT[:, :, tt * 128:(tt + 1) * 128], pv[:, 0:512].rearrange("p (c t) -> p c t", t=128), [Tt_], [Tq if wi_ == 0 else Tk])
            for tt in range(NT + 1):
                if tt < NT:
                    first(tt)
                if tt >= 1:
                    second(tt - 1)
        if "proj" in dbg:
            return
        s.barrier()
        self.a_off = mark
        NP = 4
        pT = [self.a_get([1, 512], BF16)[:, 0, :] for _ in range(NP)]
        TpT = [T() for _ in range(NP)]
        o0 = self.a_get([4, 128], F32)
        To0 = T()
        sm = self.a_get([1, 8], F32); Tsm = T()
        osq = self.a_get([1, 128], F32)
        ssA = self.a_get([NT, 4], F32); TssA = T()
        lam_init = 0.8 - 0.6 * float(np.exp(-0.3 * l))

        def evac(hm, qt, acc, Tacc):
            h, m = hm // 2, hm % 2
            j = qt % 4
            self.recip(sm[:, 0, 0:1], acc[:, 128:129], [Tacc], [Tsm])
            if m == 0:
                self.ts("dve", o0[:, j, :], acc[:, 0:128], sm[:, 0, 0:1], ALU.mult, [Tacc, Tsm], [To0])
            else:
                self.tt("dve", sm[:, 0, 1:2], sm[:, 0, 0:1], self.lam[:, 2 * l + 1:2 * l + 2], ALU.mult, [Tsm, self.Tlam], [Tsm])
                self.stt("dve", o0[:, j, :], acc[:, 0:128], sm[:, 0, 1:2], o0[:, j, :], ALU.mult, ALU.add, [Tacc, Tsm, To0], [To0])
                self.tt("dve", osq[:, 0, :], o0[:, j, :], o0[:, j, :], ALU.mult, [To0], [Tsm])
                self.red("dve", ssA[:, qt, h:h + 1], osq[:, 0, :], ALU.add, [Tsm], [TssA])
                self.cp("pool", self.o_all[:, qt, h * 128:(h + 1) * 128], o0[:, j, :], [To0], [self.To[qt][0]])
        self.causal_attn_a(qT, kT, V4, Tq, Tk, Tv, evac, pT, TpT)
        ssf = ssA.rearrange("p t c -> p (t c)")
        self.rsqrt(ssf, ssf, 1.0 / 128.0, [TssA], [TssA])
        for qt in range(NT):
            ov = self.o_all[:, qt, 0:512].rearrange("p (h c) -> p h c", c=128)
            self.tt("dve", ov, ov, ssA[:, qt, :].unsqueeze(2).broadcast_to([128, 4, 128]), ALU.mult, [self.To[qt][0], TssA], [self.To[qt][0]])
            self.tt("pool", ov, ov, self.sublng[:, :].unsqueeze(1).broadcast_to([128, 4, 128]), ALU.mult, [self.To[qt][0], self.Tlp], [self.To[qt][0]])

    def causal_attn_a(self, qT, kT, V4, Tq, Tk, Tv, evac, pT, TpT):
        ST = [5, 6, 7]
        ACC = [1, 2, 3, 4]
        LA = 3
        NP = len(pT)
        scale = 64 ** -0.5
        iters = []
        for h in range(4):
            for qc in range(4):
                for m in range(2):
                    for kt in range(4 * qc + 4):
                        iters.append((h, qc, m, kt))
        accs = [(self.ps[ACC[j]][:, 0:129], self.Tps[ACC[j]]) for j in range(4)]

        def stage1(i):
            h, qc, m, kt = iters[i]
            q_ = qT[64 * m:64 * m + 64, h, :]
            k_ = kT[64 * m:64 * m + 64, h, :]
            q0 = max(qc * 512, kt * 128)
            n = (qc + 1) * 512 - q0
            bank = ST[i % 3]; pi = i % NP
            sps = self.ps[bank]; Tsp = self.Tps[bank]
            diag = kt >= 4 * qc
            if diag and n > 128:
                self.mm(sps[:, 128:n], k_[:, kt * 128:(kt + 1) * 128], q_[:, q0 + 128:q0 + n], True, True, [Tq, Tk], [Tsp])
            if diag:
                self.mm(sps[:, 0:128], k_[:, kt * 128:(kt + 1) * 128], q_[:, q0:q0 + 128], True, False, [Tq, Tk], [Tsp])
                self.mm(sps[:, 0:128], self.identb[:], self.m_le4[:, 0:128], False, True, [self.Tcb], [Tsp])
            else:
                self.mm(sps[:, 0:n], k_[:, kt * 128:(kt + 1) * 128], q_[:, q0:q0 + n], True, True, [Tq, Tk], [Tsp])
            self.act(pT[pi][:, 0:n], sps[:, 0:n], AF.Exp, [Tsp], [TpT[pi]], scale=scale)

        def stage2(i):
            h, qc, m, kt = iters[i]
            q0 = max(qc * 512, kt * 128)
            pi = i % NP
            for j in range(4):
                qt = qc * 4 + j
                if qt < kt:
                    continue
                c0 = qt * 128 - q0
                acc, Tacc = accs[j]
                self.mm(acc, pT[pi][:, c0:c0 + 128], V4[:, kt, h, :], kt == 0, kt == qt, [TpT[pi], Tv], [Tacc])
                if kt == qt:
                    evac(h * 2 + m, qt, acc, Tacc)
        n_it = len(iters)
        for i in range(n_it + LA):
            if i < n_it:
                stage1(i)
            if i - LA >= 0:
                stage2(i - LA)

    def mixer_b(self, b, l):
        s = self.s
        self.a_reset()
        cst = self.cst
        bd = self.a_get([NT, 8], F32)
        beta = self.a_get([NT, 4], F32)
        g = self.a_get([NT, 4], F32)
        gcum = self.a_get([NT, 4], F32)
        egc = self.a_get([NT, 4], F32)
        egl = self.a_get([NT, 4], F32)
        kdec = self.a_get([NT, 4], F32)
        ea = self.a_get([1, 4], F32)
        ssB = self.a_get([NT, 4], F32); TssB = T()
        Tsc = T()
        wv, Tw_ = self.wload(self.win_cols(l, OFF_BD, 8), [KC, 8])
        ps = self.ps[0]; Tp = self.Tps[0]
        for tt in range(NT):
            for kc in range(KC):
                self.mm(ps[:, tt * 8:(tt + 1) * 8], self.hT[:, kc, tt * 128:(tt + 1) * 128], wv[:, kc, :], kc == 0, kc == KC - 1, [self.ThT[tt], Tw_], [Tp])
        self.cp("act", bd, ps[:, 0:128].rearrange("p (t c) -> p t c", c=8), [Tp], [Tsc])
        self.act(beta, bd[:, :, 0:4], AF.Sigmoid, [Tsc], [Tsc])
        self.tt("dve", g, bd[:, :, 4:8], self.dtb_bc[:, :].unsqueeze(1).broadcast_to([128, NT, 4]), ALU.add, [Tsc, self.Tlp], [Tsc])
        self.act(g, g, AF.Exp, [Tsc], [Tsc])
        self.act(g, g, AF.Ln, [Tsc, self.Tcb], [Tsc], bias=self.epsc[:, 2:3])
        self.act(ea[:, 0, :], self.alog_bc[:, :], AF.Exp, [self.Tlp], [Tsc])
        self.tt("dve", g, g, ea[:, 0, :].unsqueeze(1).broadcast_to([128, NT, 4]), ALU.mult, [Tsc], [Tsc])
        self.ts("dve", g, g, -1.0, ALU.mult, [Tsc], [Tsc])
        g2 = g.rearrange("p t c -> p (t c)")
        p1 = self.ps[1]; Tp1 = self.Tps[1]
        p2 = self.ps[2]; Tp2 = self.Tps[2]
        self.mm(p1[:, 0:64], cst[:, C_LE:C_LE + 128], g2, True, True, [self.Tcst, Tsc], [Tp1])
        self.mm(p2[:, 0:64], cst[:, C_ONE:C_ONE + 128], g2, True, True, [self.Tcst, Tsc], [Tp2])
        self.cp("dve", gcum.rearrange("p t c -> p (t c)"), p1[:, 0:64], [Tp1], [Tsc])
        self.act(egc.rearrange("p t c -> p (t c)"), p1[:, 0:64], AF.Exp, [Tp1], [Tsc])
        self.act(egl.rearrange("p t c -> p (t c)"), p2[:, 0:64], AF.Exp, [Tp2], [Tsc])
        self.tt("dve", kdec.rearrange("p t c -> p (t c)"), p2[:, 0:64], gcum.rearrange("p t c -> p (t c)"), ALU.subtract, [Tp2, Tsc], [Tsc])
        self.act(kdec, kdec, AF.Exp, [Tsc], [Tsc])
        mark_pair = self.a_off
        for pr in range(2):
            s.barrier()
            self.a_off = mark_pair
            QT = self.a_get([2, S_], BF16)
            KT = self.a_get([2, S_], BF16)
            Kd = self.a_get([NT, 256], BF16)
            VK = self.a_get([NT, 512], BF16)
            Tq = T(); Tk = T(); Tkd = T(); Tvk = T()
            mark = self.a_off
            HS = 1024
            xb = self.a_get([1, 3 + HS], F32)[:, 0, :]; Txb = T()
            yb = self.a_get([1, HS], F32)[:, 0, :]; Ty = T()
            sqb = self.a_get([1, HS], BF16)[:, 0, :]; Tsq = T()
            rr = self.a_get([1, 512], F32)[:, 0, :]; Trr = T()
            vT = self.a_get([1, HS], BF16)[:, 0, :]; TvT = T()
            for kind in range(3):
                for hl in range(2):
                    h = 2 * pr + hl
                    cidx = kind * 4 + h
                    wv, Tw_ = self.wload(self.win_cols(l, OFF_QB + cidx * 128, 128), [KC, 128])
                    self.memset("pool", xb[:, 0:3], 0.0, [Txb])
                    for hf in range(2):
                        for tc in range(2):
                            t0 = hf * HS + tc * 512
                            bank = tc
                            pp = self.ps[bank]; Tpp = self.Tps[bank]
                            for kc in range(KC):
                                self.mm(pp[:], wv[:, kc, :], self.hT[:, kc, t0:t0 + 512], kc == 0, kc == KC - 1, self.ThT[t0 // 128:t0 // 128 + 4] + [Tw_], [Tpp])
                            self.cp("act", xb[:, 3 + tc * 512:3 + (tc + 1) * 512], pp[:], [Tpp], [Txb])
                        self.ts("dve", yb, xb[:, 0:HS], self.convw[:, cidx, 0:1], ALU.mult, [Txb, self.Tlp], [Ty])
                        for j in range(1, 4):
                            self.stt("dve", yb, xb[:, j:j + HS], self.convw[:, cidx, j:j + 1], yb, ALU.mult, ALU.add, [Txb, Ty, self.Tlp], [Ty])
                        if hf == 0:
                            self.cp("pool", xb[:, 0:3], xb[:, HS:HS + 3], [Txb], [Txb])
                        if kind == 2:
                            self.act(vT, yb, AF.Silu, [Ty], [TvT])
                        else:
                            self.act(yb, yb, AF.Silu, [Ty], [Ty])
                            self.act(sqb, yb, AF.Square, [Ty], [Tsq])
                            dstT = QT if kind == 0 else KT
                            Td = Tq if kind == 0 else Tk
                            for tc in range(2):
                                bank = 2 + tc
                                pp = self.ps[bank]; Tpp = self.Tps[bank]
                                self.mm(pp[:], self.onesb[:], sqb[:, tc * 512:(tc + 1) * 512], True, True, [Tsq, self.Tcb], [Tpp])
                                if kind == 0:
                                    self.act(rr, pp[:], AF.Sqrt, [Tpp, self.Tcb], [Trr], bias=self.epsc[:, 1:2], scale=128.0)
                                else:
                                    self.act(rr, pp[:], AF.Sqrt, [Tpp, self.Tcb], [Trr], bias=self.epsc[:, 0:1], scale=1.0)
                                self.recip(rr, rr, [Trr], [Trr])
                                self.tt("dve", dstT[:, hl, hf * HS + tc * 512:hf * HS + (tc + 1) * 512], yb[:, tc * 512:(tc + 1) * 512], rr, ALU.mult, [Ty, Trr], [Td])
                        if kind >= 1:
                            for t4 in range(2):
                                tb = 4 + t4
                                pv = self.ps[tb][:].bitcast(BF16); Tt_ = self.Tps[tb]
                                for j in range(4):
                                    tl = t4 * 4 + j
                                    tile_ = hf * 8 + tl
                                    src_ = KT[:, hl, tile_ * 128:(tile_ + 1) * 128] if kind == 1 else vT[:, tl * 128:(tl + 1) * 128]
                                    self.tr(pv[:, j * 128:(j + 1) * 128], src_, self.identb[:], [Tk if kind == 1 else TvT, self.Tcb], [Tt_])
                                if kind == 1:
                                    for j in range(4):
                                        tile_ = hf * 8 + t4 * 4 + j
                                        self.ts("dve", Kd[:, tile_, hl * 128:(hl + 1) * 128], pv[:, j * 128:(j + 1) * 128], kdec[:, tile_, h:h + 1], ALU.mult, [Tt_, Tsc], [Tkd])
                                    for j in range(4):
                                        tile_ = hf * 8 + t4 * 4 + j
                                        self.act(VK[:, tile_, hl * 256 + 128:hl * 256 + 256], pv[:, j * 128:(j + 1) * 128], AF.Copy, [Tt_, Tsc], [Tvk], scale=egc[:, tile_, h:h + 1])
                                else:
                                    for j in range(4):
                                        tile_ = hf * 8 + t4 * 4 + j
                                        self.cp("act", VK[:, tile_, hl * 256:hl * 256 + 128], pv[:, j * 128:(j + 1) * 128], [Tt_], [Tvk])
            s.barrier()
            self.a_off = mark
            def f2(dt=F32):
                return self.a_get([1, 256], dt)[:, 0, :]
            A1 = f2(); A2 = f2(); M1 = A2; Uq = f2(); Lq = f2()
            Pb = [f2(), f2()]; Ptb = [f2(), f2()]; Xb_ = [f2(), f2()]; Xtb = [f2(), f2()]
            Xbf = f2(BF16); wtm = f2(BF16); vn = f2(BF16)
            qkT2 = [f2(BF16), f2(BF16)]; uu2 = [f2(), f2()]; wT2 = [f2(BF16), f2(BF16)]
            O2 = f2()
            Sf = f2(); Sb = f2(BF16)
            TA1 = T(); TA2 = T(); TM1 = TA2; TU = T(); TL = T(); TP = [T(), T()]; TPt = [T(), T()]; TX = [T(), T()]; TXt = [T(), T()]
            TXbf = T(); Tqk2 = [T(), T()]; Tuu2 = [T(), T()]; Twtm = T(); TwT2 = [T(), T()]; Tvn = T(); TS = T(); TSb = T()
            TO2 = T()
            self.memset("pool", Sf, 0.0, [TS])
            self.memset("pool", Sb, 0.0, [TSb])
            idf = cst[:, C_ID:C_ID + 128]
            onef = cst[:, C_ONE:C_ONE + 128]
            H2 = [slice(0, 128), slice(128, 256)]

            def bc(off):
                return cst[:, off:off + 128].unsqueeze(1).broadcast_to([128, 2, 128])

            def v3(ap):
                return ap.rearrange("p (h c) -> p h c", c=128)

            def pre(t, so):
                ts_ = slice(t * 128, (t + 1) * 128)
                qkT, Tqk = qkT2[so], Tqk2[so]
                uu, Tuu = uu2[so], Tuu2[so]
                wT, TwT = wT2[so], TwT2[so]
                b0 = self.ps[0]; Tb0 = self.Tps[0]
                b1 = self.ps[1]; Tb1 = self.Tps[1]
                for hl in range(2):
                    self.mm(b0[:, hl * 128:(hl + 1) * 128], KT[:, hl, ts_], KT[:, hl, ts_], True, True, [Tk], [Tb0])
                for hl in range(2):
                    self.mm(b0[:, 256 + hl * 128:256 + (hl + 1) * 128], KT[:, hl, ts_], QT[:, hl, ts_], True, True, [Tk, Tq], [Tb0])
                for hl in range(2):
                    h = 2 * pr + hl
                    self.ts("pool", A1[:, hl * 128:(hl + 1) * 128], idf, gcum[:, t, h:h + 1], ALU.mult, [self.Tcst, Tsc], [TA1])
                yield
                for hl in range(2):
                    self.mm(b1[:, hl * 128:(hl + 1) * 128], onef, A1[:, hl * 128:(hl + 1) * 128], True, True, [self.Tcst, TA1], [Tb1])
                for hl in range(2):
                    h = 2 * pr + hl
                    self.stt("dve", A2[:, hl * 128:(hl + 1) * 128], b1[:, hl * 128:(hl + 1) * 128], gcum[:, t, h:h + 1], cst[:, C_NLE:C_NLE + 128],
                             ALU.subtract, ALU.add, [Tb1, Tsc, self.Tcst], [TA2])
                self.act(M1, A2, AF.Exp, [TA2], [TM1])
                yield
                self.tt("dve", qkT, b0[:, 256:512], M1, ALU.mult, [Tb0, TM1], [Tqk])
                self.tt("dve", Uq, b0[:, 0:256], M1, ALU.mult, [Tb0, TM1], [TU])
                for hl in range(2):
                    h = 2 * pr + hl
                    sl = H2[hl]
                    self.stt("dve", Uq[:, sl], Uq[:, sl], beta[:, t, h:h + 1], cst[:, C_LT:C_LT + 128], ALU.mult, ALU.mult, [TU, Tsc, self.Tcst], [TU])
                yield
                for sl in H2:
                    self.tr(b1[:, sl], Uq[:, sl], idf, [TU, self.Tcst], [Tb1])
                self.cp("act", Lq, b1[:, 0:256], [Tb1], [TL])
                yield
                bP = self.ps[0]; TbP = self.Tps[0]
                bPt = self.ps[1]; TbPt = self.Tps[1]
                bX = self.ps[2]; TbX = self.Tps[2]
                bXt = self.ps[3]; TbXt = self.Tps[3]
                Nn, TN = Pb[0], TP[0]
                Nt, TNt = Ptb[0], TPt[0]
                self.tt("pool", v3(Nn), v3(Uq), bc(C_B8), ALU.mult, [TU, self.Tcst], [TN])
                self.tt("pool", v3(Nt), v3(Lq), bc(C_B8), ALU.mult, [TL, self.Tcst], [TNt])
                E, TE = Xb_[0], TX[0]
                Et, TEt = Xtb[0], TXt[0]
                for sl in H2:
                    self.stt("dve", E[:, sl], Nn[:, sl], -1.0, idf, ALU.mult, ALU.add, [TN, self.Tcst], [TE])
                    self.stt("dve", Et[:, sl], Nt[:, sl], -1.0, idf, ALU.mult, ALU.add, [TNt, self.Tcst], [TEt])
                yield
                for sl in H2:
                    self.mm(bP[:, sl], Nt[:, sl], Nn[:, sl], True, True, [TN, TNt], [TbP])
                for sl in H2:
                    self.mm(bPt[:, sl], Nn[:, sl], Nt[:, sl], True, True, [TN, TNt], [TbPt])
                P1, TP1 = Pb[1], TP[1]
                Pt1, TPt1 = Ptb[1], TPt[1]
                self.cp("act", P1, bP[:, 0:256], [TbP], [TP1])
                self.cp("act", Pt1, bPt[:, 0:256], [TbPt], [TPt1])
                yield
                xs = {"xi": 0}

                def xupd(Pt_, TPt_):
                    xi = xs["xi"]
                    Ec, TEc = Xb_[xi], TX[xi]
                    Etc, TEtc = Xtb[xi], TXt[xi]
                    for sl in H2:
                        self.mm(bX[:, sl], Pt_[:, sl], Ec[:, sl], True, True, [TPt_, TEc], [TbX])
                    for sl in H2:
                        self.mm(bXt[:, sl], Ec[:, sl], Pt_[:, sl], True, True, [TPt_, TEc], [TbXt])
                    self.tt("dve", Xb_[1 - xi], bX[:, 0:256], Ec, ALU.add, [TbX, TEc], [TX[1 - xi]])
                    self.tt("dve", Xtb[1 - xi], bXt[:, 0:256], Etc, ALU.add, [TbXt, TEtc], [TXt[1 - xi]])
                    xs["xi"] = 1 - xi
                xupd(Pt1, TPt1)
                yield
                for sl in H2:
                    self.mm(bPt[:, sl], P1[:, sl], Pt1[:, sl], True, True, [TP1, TPt1], [TbPt])
                Pt2, TPt2 = Ptb[0], TPt[0]
                self.cp("act", Pt2, bPt[:, 0:256], [TbPt], [TPt2])
                yield
                xupd(Pt2, TPt2)
                yield
                Ff, TF = Pb[0], TP[0]
                G1, TG1 = Ptb[1], TPt[1]
                for li, woff in enumerate((C_W8, C_W16, C_W32, C_W64)):
                    xi = xs["xi"]
                    Ec, TEc = Xb_[xi], TX[xi]
                    Etc, TEtc = Xtb[xi], TXt[xi]
                    self.tt("pool", v3(Ff), v3(Uq), bc(woff), ALU.mult, [TU, self.Tcst], [TF])
                    for sl in H2:
                        self.mm(bP[:, sl], Ff[:, sl], Etc[:, sl], True, True, [TF, TEtc], [TbP])
                    self.cp("act", G1, bP[:, 0:256], [TbP], [TG1])
                    yield
                    for sl in H2:
                        self.mm(bX[:, sl], G1[:, sl], Ec[:, sl], True, True, [TG1, TEc], [TbX])
                    self.stt("dve", Xb_[1 - xi], bX[:, 0:256], -1.0, Ec, ALU.mult, ALU.add, [TbX, TEc], [TX[1 - xi]])
                    if li < 3:
                        for sl in H2:
                            self.mm(bXt[:, sl], Ec[:, sl], G1[:, sl], True, True, [TG1, TEc], [TbXt])
                        self.stt("dve", Xtb[1 - xi], bXt[:, 0:256], -1.0, Etc, ALU.mult, ALU.add, [TbXt, TEtc], [TXt[1 - xi]])
                    xs["xi"] = 1 - xi
                    yield
                xi = xs["xi"]
                self.cp("pool", Xbf, Xb_[xi], [TX[xi]], [TXbf])
                b3 = self.ps[3]; Tb3 = self.Tps[3]
                for hl in range(2):
                    self.mm(b3[:, hl * 256:(hl + 1) * 256], Xbf[:, hl * 128:(hl + 1) * 128], VK[:, t, hl * 256:(hl + 1) * 256], True, True, [TXbf, Tvk], [Tb3])
                for hl in range(2):
                    h = 2 * pr + hl
                    self.act(uu[:, hl * 128:(hl + 1) * 128], b3[:, hl * 256:hl * 256 + 128], AF.Copy, [Tb3, Tsc], [Tuu], scale=beta[:, t, h:h + 1])
                    self.act(wtm[:, hl * 128:(hl + 1) * 128], b3[:, hl * 256 + 128:hl * 256 + 256], AF.Copy, [Tb3, Tsc], [Twtm], scale=beta[:, t, h:h + 1])
                yield
                b2 = self.ps[2]; Tb2 = self.Tps[2]
                pv2 = b2[:].bitcast(BF16)
                for hl in range(2):
                    self.tr(pv2[:, hl * 128:(hl + 1) * 128], wtm[:, hl * 128:(hl + 1) * 128], self.identb[:], [Twtm, self.Tcb], [Tb2])
                self.cp("dve", wT, pv2[:, 0:256], [Tb2], [TwT])
                yield

            def scan(t, so):
                ts_ = slice(t * 128, (t + 1) * 128)
                qkT, Tqk = qkT2[so], Tqk2[so]
                uu, Tuu = uu2[so], Tuu2[so]
                wT, TwT = wT2[so], TwT2[so]
                b4 = self.ps[4]; Tb4 = self.Tps[4]
                b5 = self.ps[5]; Tb5 = self.Tps[5]
                b6 = self.ps[6]; Tb6 = self.Tps[6]
                b7 = self.ps[7]; Tb7 = self.Tps[7]
                for sl in H2:
                    self.mm(b4[:, sl], wT[:, sl], Sb[:, sl], True, True, [TwT, TSb], [Tb4])
                for hl in range(2):
                    self.mm(b5[:, H2[hl]], QT[:, hl, ts_], Sb[:, H2[hl]], True, True, [Tq, TSb], [Tb5])
                self.stt("dve", vn, b4[:, 0:256], -1.0, uu, ALU.mult, ALU.add, [Tb4, Tuu], [Tvn])
                yield
                for sl in H2:
                    self.mm(b7[:, sl], Kd[:, t, sl], vn[:, sl], True, True, [Tkd, Tvn], [Tb7])
                for sl in H2:
                    self.mm(b6[:, sl], qkT[:, sl], vn[:, sl], True, True, [Tqk, Tvn], [Tb6])
                for hl in range(2):
                    h = 2 * pr + hl
                    sl = H2[hl]
                    self.act(Sf[:, sl], Sf[:, sl], AF.Copy, [TS, Tsc], [TS], scale=egl[:, t, h:h + 1])
                    self.act(O2[:, sl], b5[:, sl], AF.Copy, [Tb5, Tsc], [TO2], scale=egc[:, t, h:h + 1])
                yield
                self.tt("dve", Sf, b7[:, 0:256], Sf, ALU.add, [Tb7, TS], [TS])
                self.cp("pool", Sb, Sf, [TS], [TSb])
                yield
                self.tt("dve", O2, b6[:, 0:256], O2, ALU.add, [Tb6, TO2], [TO2])
                self.tt("dve", vn, O2, O2, ALU.mult, [TO2], [Tvn])
                self.red("dve", ssB[:, t, 2 * pr:2 * pr + 2], vn.rearrange("p (h c) -> p h c", c=128), ALU.add, [Tvn], [TssB])
                self.cp("pool", self.o_all[:, t, 512 + 2 * pr * 128:512 + (2 * pr + 2) * 128], O2, [TO2], [self.To[t][1]])
                yield

            for _ in pre(0, 0):
                pass
            for t in range(NT):
                gens = [scan(t, t % 2)]
                if t + 1 < NT:
                    gens.append(pre(t + 1, (t + 1) % 2))
                while gens:
                    for g_ in list(gens):
                        try:
                            next(g_)
                        except StopIteration:
                            gens.remove(g_)
        s.barrier()
        ssf = ssB.rearrange("p t c -> p (t c)")
        self.rsqrt(ssf, ssf, 1.0 / 128.0, [TssB], [TssB])
        for t in range(NT):
            ov = self.o_all[:, t, 512:1024].rearrange("p (h c) -> p h c", c=128)
            self.tt("dve", ov, ov, ssB[:, t, :].unsqueeze(2).broadcast_to([128, 4, 128]), ALU.mult, [self.To[t][1], TssB], [self.To[t][1]])
            self.tt("pool", ov, ov, self.onormb[:, :].unsqueeze(1).broadcast_to([128, 4, 128]), ALU.mult, [self.To[t][1], self.Tlp], [self.To[t][1]])
        s.barrier()

    def mixer_c(self, b, l):
        s = self.s
        self.a_reset()
        cqnT = self.a_get([2, S_], BF16)
        ckvnT = self.a_get([1, S_], BF16)
        kper = self.a_get([NT, 32], BF16)
        Tcq = T(); Tkr = T()
        mark0 = self.a_off
        NB = 2
        xc = [self.a_get([1, 416], F32) for _ in range(NB)]; Txc = [T() for _ in range(NB)]
        cn = [self.a_get([1, 384], BF16) for _ in range(NB)]; Tcn = [T() for _ in range(NB)]
        sq = self.a_get([1, 256], F32)
        sm = [self.a_get([1, 4], F32) for _ in range(NB)]; Tsm = [T() for _ in range(NB)]
        tmp = [self.a_get([16, 16], F32), self.a_get([16, 16], F32), T()]
        wv, Tw_ = self.wload(self.win_cols(l, OFF_C, 416), [KC, 416])
        def first1(tt):
            i = tt % NB
            ps = self.ps[tt % 2]; Tp = self.Tps[tt % 2]
            self.proj_tm(tt, wv, Tw_, 416, ps[:, 0:416], Tp)
            self.cp("act", xc[i][:, 0, :], ps[:, 0:416], [Tp], [Txc[i]])
            self.act(sq[:, 0, 0:256], xc[i][:, 0, 0:256], AF.Square, [Txc[i]], [Tsm[i]], accum=sm[i][:, 0, 0:1])
            self.act(sq[:, 0, 0:128], xc[i][:, 0, 256:384], AF.Square, [Txc[i]], [Tsm[i]], accum=sm[i][:, 0, 1:2])
            self.rsqrt(sm[i][:, 0, 2:3], sm[i][:, 0, 0:1], 1.0 / 256.0, [Tsm[i]], [Tsm[i]])
            self.rsqrt(sm[i][:, 0, 3:4], sm[i][:, 0, 1:2], 1.0 / 128.0, [Tsm[i]], [Tsm[i]])
            self.ts("dve", cn[i][:, 0, 0:256], xc[i][:, 0, 0:256], sm[i][:, 0, 2:3], ALU.mult, [Txc[i], Tsm[i]], [Tcn[i]])
            self.ts("dve", cn[i][:, 0, 256:384], xc[i][:, 0, 256:384], sm[i][:, 0, 3:4], ALU.mult, [Txc[i], Tsm[i]], [Tcn[i]])
            x3 = xc[i][:, 0, 384:416].rearrange("p (m c) -> p m c", c=32)
            o3 = kper[:, tt, :].rearrange("p (m c) -> p m c", c=32)
            self.rope_tm(tt, x3, o3, 1, 16, [Txc[i]], [Tkr], tmp)

        def second1(tt):
            i = tt % NB
            tb = 2 + (tt % 2)
            pv = self.ps[tb][:].bitcast(BF16); Tt_ = self.Tps[tb]
            for c in range(3):
                self.tr(pv[:, c * 128:(c + 1) * 128], cn[i][:, 0, c * 128:(c + 1) * 128], self.identb[:], [Tcn[i], self.Tcb], [Tt_])
            for c in range(2):
                self.ts("dve", cqnT[:, c, tt * 128:(tt + 1) * 128], pv[:, c * 128:(c + 1) * 128], self.qnw[:, c:c + 1], ALU.mult, [Tt_, self.Tlp], [Tcq])
            self.act(ckvnT[:, 0, tt * 128:(tt + 1) * 128], pv[:, 256:384], AF.Copy, [Tt_, self.Tlp], [Tcq], scale=self.kvnw[:, 0:1])
        for tt in range(NT + 1):
            if tt < NT:
                first1(tt)
            if tt >= 1:
                second1(tt - 1)
        s.barrier()
        self.a_off = mark0
        QT = self.a_get([4, S_], BF16)
        KT = self.a_get([4, S_], BF16)
        V = self.a_get([NT, 4 * 65], BF16)
        V4 = V.rearrange("p t (h c) -> p t h c", c=65)
        mark1 = self.a_off
        for g2 in range(2):
            self.a_off = mark1
            Tq = T(); Tk = T(); Tv = T()
            self.memset("pool", V4[:, :, :, 64:65], 1.0, [Tv])
            xq = [self.a_get([1, 384], F32) for _ in range(NB)]; Txq = [T() for _ in range(NB)]
            qtm = [self.a_get([1, 384], BF16) for _ in range(NB)]; Tqt = [T() for _ in range(NB)]
            ktm = [self.a_get([1, 384], BF16) for _ in range(NB)]; Tkt = [T() for _ in range(NB)]
            tmp = [self.a_get([16, 16], F32), self.a_get([16, 16], F32), T()]
            wq, Twq = self.wload(self.w_uq[l].rearrange("(kc p) c -> p kc c", p=128)[:, :, g2 * 384:(g2 + 1) * 384], [2, 384])
            wk, Twk = self.wload(self.w_ukv[l].rearrange("(kc p) c -> p kc c", p=128)[:, :, g2 * 512:(g2 + 1) * 512], [1, 512])
            def first2(tt, wq=wq, Twq=Twq, wk=wk, Twk=Twk, xq=xq, Txq=Txq, qtm=qtm, Tqt=Tqt, ktm=ktm, Tkt=Tkt, tmp=tmp, Tv=Tv):
                i = tt % NB
                ps = self.ps[tt % 2]; Tp = self.Tps[tt % 2]
                for c in range(2):
                    self.mm(ps[:, 0:384], cqnT[:, c, tt * 128:(tt + 1) * 128], wq[:, c, :], c == 0, c == 1, [Tcq, Twq], [Tp])
                self.cp("act", xq[i][:, 0, :], ps[:, 0:384], [Tp], [Txq[i]])
                x3 = xq[i][:, 0, :].rearrange("p (m c) -> p m c", c=96)
                o3 = qtm[i][:, 0, :].rearrange("p (m c) -> p m c", c=96)
                self.cp("pool", o3[:, :, 0:64], x3[:, :, 0:64], [Txq[i]], [Tqt[i]])
                self.rope_tm(tt, x3[:, :, 64:96], o3[:, :, 64:96], 4, 16, [Txq[i]], [Tqt[i]], tmp)
                pk = self.ps[4 + (tt % 2)]; Tpk = self.Tps[4 + (tt % 2)]
                self.mm(pk[:], ckvnT[:, 0, tt * 128:(tt + 1) * 128], wk[:, 0, :], True, True, [Tcq, Twk], [Tpk])
                pk3 = pk[:].rearrange("p (h c) -> p h c", c=128)
                self.cp("act", V4[:, tt, :, 0:64], pk3[:, :, 64:128], [Tpk], [Tv])
                k3 = ktm[i][:, 0, :].rearrange("p (m c) -> p m c", c=96)
                self.cp("act", k3[:, :, 0:64], pk3[:, :, 0:64], [Tpk], [Tkt[i]])
                self.cp("pool", k3[:, :, 64:96], kper[:, tt, :].unsqueeze(1).broadcast_to([128, 4, 32]), [Tkr], [Tkt[i]])

            def second2(tt, qtm=qtm, Tqt=Tqt, ktm=ktm, Tkt=Tkt, Tq=Tq, Tk=Tk):
                i = tt % NB
                tb = 2 + (tt % 2)
                pv = self.ps[tb][:].bitcast(BF16); Tt_ = self.Tps[tb]
                for h in range(4):
                    self.tr(pv[0:96, h * 128:(h + 1) * 128], qtm[i][:, 0, h * 96:(h + 1) * 96], self.identb[:], [Tqt[i], self.Tcb], [Tt_])
                self.cp("dve", QT[0:96, :, tt * 128:(tt + 1) * 128], pv[0:96, 0:512].rearrange("p (c t) -> p c t", t=128), [Tt_], [Tq])
                tb2 = 6 + (tt % 2)
                pv2 = self.ps[tb2][:].bitcast(BF16); Tt2 = self.Tps[tb2]
                for h in range(4):
                    self.tr(pv2[0:96, h * 128:(h + 1) * 128], ktm[i][:, 0, h * 96:(h + 1) * 96], self.identb[:], [Tkt[i], self.Tcb], [Tt2])
                self.cp("dve", KT[0:96, :, tt * 128:(tt + 1) * 128], pv2[0:96, 0:512].rearrange("p (c t) -> p c t", t=128), [Tt2], [Tk])
            for tt in range(NT + 1):
                if tt < NT:
                    first2(tt)
                if tt >= 1:
                    second2(tt - 1)
            s.barrier()
            self.a_off = mark1
            NP = 4
            pT = [self.a_get([1, 512], BF16)[:, 0, :] for _ in range(NP)]
            TpT = [T() for _ in range(NP)]
            rz = self.a_get([1, 4], F32); Trz = T()

            def evac(h, qt, acc, Tacc, g2=g2, rz=rz, Trz=Trz):
                self.recip(rz[:, 0, 0:1], acc[:, 64:65], [Tacc], [Trz])
                hg = 4 * g2 + h
                self.ts("dve", self.o_all[:, qt, 1024 + hg * 64:1024 + (hg + 1) * 64], acc[:, 0:64], rz[:, 0, 0:1], ALU.mult, [Tacc, Trz], [self.To[qt][2]])
            self.causal_attn(4, lambda h: QT[0:96, h, :], lambda h: KT[0:96, h, :], lambda h, kt: V4[:, kt, h, :], 64, 96 ** -0.5,
                             Tq, Tk, Tv, evac, pT, TpT)
            s.barrier()


    def tok_subset(self, g, blk, kc):
        if g == 0:
            return self.hT[:, kc, blk * 128:(blk + 1) * 128]
        if g == 1:
            r, n = blk // 4, blk % 4
            return self.hT[:, kc, :].rearrange("p (j r) -> p r j", r=4)[:, r, n * 128:(n + 1) * 128]
        return self.hT[:, kc, :].rearrange("p (j r) -> p r j", r=16)[:, blk, :]

    def mixer_d(self, b, l):
        s = self.s
        wd_all = self.w_d[l].rearrange("(kc p) c -> p kc c", p=128)
        scale = 64 ** -0.5
        for pr in range(4):
            self.a_reset()
            QT = self.a_get([3, S_], BF16)
            KT = self.a_get([3, S_], BF16)
            Vd = self.a_get([48, 130], BF16)
            Vd4 = Vd.rearrange("p n (h c) -> p n h c", c=65)
            UTf = self.a_get([1, S_], F32)
            UT = UTf[0:65, 0, :]
            Tq = T(); Tk = T(); Tv = T(); TU = T()
            self.memset("pool", Vd4[:, :, :, 64:65], 1.0, [Tv])
            mark = self.a_off
            NB = 2
            xf = [self.a_get([1, 384], F32) for _ in range(NB)]; Txf = [T() for _ in range(NB)]
            qd = [self.a_get([1, 384], BF16) for _ in range(NB)]; Tqd = [T() for _ in range(NB)]
            tmp = [self.a_get([16, 16], F32), self.a_get([16, 16], F32), T()]
            for wi_ in range(2):
                wv, Tw_ = self.wload(wd_all[:, :, pr * 1152 + wi_ * 384:pr * 1152 + (wi_ + 1) * 384], [KC, 384])
                def firstd(tt, wv=wv, Tw_=Tw_):
                    i = tt % NB
                    ps = self.ps[tt % 2]; Tp = self.Tps[tt % 2]
                    self.proj_tm(tt, wv, Tw_, 384, ps[:, 0:384], Tp)
                    self.cp("act", xf[i][:, 0, :], ps[:, 0:384], [Tp], [Txf[i]])
                    x3 = xf[i][:, 0, :].rearrange("p (m c) -> p m c", c=64)
                    o3 = qd[i][:, 0, :].rearrange("p (m c) -> p m c", c=64)
                    self.cp("pool", o3[:, :, 16:64], x3[:, :, 16:64], [Txf[i]], [Tqd[i]])
                    self.rope_tm(tt, x3, o3, 6, 8, [Txf[i]], [Tqd[i]], tmp)

                def secondd(tt, wi_=wi_):
                    i = tt % NB
                    tb = 2 + (tt % 2)
                    pv = self.ps[tb][:].bitcast(BF16); Tt_ = self.Tps[tb]
                    for g in range(3):
                        self.tr(pv[:, g * 128:(g + 1) * 128], qd[i][:, 0, g * 128:(g + 1) * 128], self.identb[:], [Tqd[i], self.Tcb], [Tt_])
                    dstT = QT if wi_ == 0 else KT
                    self.cp("dve", dstT[:, :, tt * 128:(tt + 1) * 128], pv[:, 0:384].rearrange("p (c t) -> p c t", t=128), [Tt_], [Tq if wi_ == 0 else Tk])
                for tt in range(NT + 1):
                    if tt < NT:
                        firstd(tt)
                    if tt >= 1:
                        secondd(tt - 1)
            wv, Tw_ = self.wload(wd_all[:, :, pr * 1152 + 768:pr * 1152 + 1152], [KC, 384])
            cnt = 0
            for g in range(3):
                for b4 in range(4):
                    bank = 4 + (cnt % 2); cnt += 1
                    ps = self.ps[bank]; Tp = self.Tps[bank]
                    for j in range(4):
                        blk = b4 * 4 + j
                        for kc in range(KC):
                            self.mm(ps[:, j * 128:(j + 1) * 128], self.tok_subset(g, blk, kc), wv[:, kc, g * 128:(g + 1) * 128], kc == 0, kc == KC - 1,
                                    self.ThT + [Tw_], [Tp])
                    self.cp("act", Vd4[:, g * 16 + b4 * 4:g * 16 + b4 * 4 + 4, :, 0:64], ps[:].rearrange("p (j h c) -> p j h c", h=2, c=64), [Tp], [Tv])
            s.barrier()
            self.a_off = mark
            NP = 5
            LA = 3
            pT = [self.a_get([1, 512], BF16)[:, 0, :] for _ in range(NP)]
            TpT = [T() for _ in range(NP)]
            rz = self.a_get([1, 4], F32); Trz = T()
            ST = [5, 6, 7]
            ACCB = [1, 2]
            TRB = [3, 4]
            st = {"acc": 0, "cur": None}

            for hh in range(2):
                h = 2 * pr + hh
                rows = slice(64 * hh, 64 * hh + 64)
                units = []

                def emit0(n0, ps_, Tp_):
                    self.cp("act", UT[:, n0 * 128:(n0 + 4) * 128], ps_[0:65, :], [Tp_], [TU])
                for m0 in range(0, 16, 2):
                    units.append(("band", QT[rows, 0, :], KT[rows, 0, :], 0, 16, m0, emit0))
                for r in range(4):
                    qv = QT[rows, 1, :].rearrange("p (j r) -> p r j", r=4)[:, r, :]
                    kv = KT[rows, 1, :].rearrange("p (j r) -> p r j", r=4)[:, r, :]
                    uv = UT.rearrange("c (j r) -> c r j", r=4)[:, r, :]

                    def emit1(n0, ps_, Tp_, uv=uv):
                        self.tt("dve", uv, ps_[0:65, :], uv, ALU.add, [Tp_, TU], [TU])
                    for m0 in range(0, 4, 2):
                        units.append(("band", qv, kv, 16 + r * 4, 4, m0, emit1))
                for r0 in range(0, 16, 4):
                    units.append(("g2", None, None, 32, 1, r0, None))

                def stage1(i):
                    kind, qv, kv, vbase, nb, m0, emit = units[i]
                    bank = ST[i % 3]; pi = i % NP
                    sps = self.ps[bank]; Tsp = self.Tps[bank]
                    if kind == "band":
                        tot = 0
                        for j in range(2):
                            m = m0 + j
                            ncol = 256 if m < nb - 1 else 128
                            self.mm(sps[:, j * 256:j * 256 + ncol], kv[:, m * 128:(m + 1) * 128], qv[:, m * 128:m * 128 + ncol], True, False, [Tq, Tk], [Tsp])
                            self.mm(sps[:, j * 256:j * 256 + ncol], self.identb[:], self.m_lg2[:, j * 256:j * 256 + ncol], False, True, [self.Tcb], [Tsp])
                            tot = j * 256 + ncol
                        self.act(pT[pi][:, 0:tot], sps[:, 0:tot], AF.Exp, [Tsp], [TpT[pi]], scale=scale)
                    else:
                        r0 = m0
                        for j in range(4):
                            r = r0 + j
                            qv2 = QT[rows, 2, :].rearrange("p (j r) -> p r j", r=16)[:, r, :]
                            kv2 = KT[rows, 2, :].rearrange("p (j r) -> p r j", r=16)[:, r, :]
                            self.mm(sps[:, j * 128:(j + 1) * 128], kv2, qv2, True, False, [Tq, Tk], [Tsp])
                            self.mm(sps[:, j * 128:(j + 1) * 128], self.identb[:], self.m_le4[:, j * 128:(j + 1) * 128], False, True, [self.Tcb], [Tsp])
                        self.act(pT[pi][:, :], sps[:, :], AF.Exp, [Tsp], [TpT[pi]], scale=scale)

                def stage2(i):
                    kind, qv, kv, vbase, nb, m0, emit = units[i]
                    pi = i % NP
                    cur = (pT[pi], TpT[pi])
                    if kind == "band":
                        prev = (pT[(i - 1) % NP], TpT[(i - 1) % NP])
                        for j in range(2):
                            n = m0 + j
                            if n % 4 == 0:
                                ab = ACCB[st["acc"] % 2]; st["acc"] += 1
                                st["cur"] = (self.ps[ab], self.Tps[ab])
                            acc = st["cur"]
                            parts = []
                            if n > 0:
                                if j == 0:
                                    parts.append((prev[0][:, 384:512], prev[1], Vd4[:, vbase + n - 1, hh, :]))
                                else:
                                    parts.append((cur[0][:, 128:256], cur[1], Vd4[:, vbase + n - 1, hh, :]))
                            parts.append((cur[0][:, j * 256:j * 256 + 128], cur[1], Vd4[:, vbase + n, hh, :]))
                            for idx, (pap, Tpp, vap) in enumerate(parts):
                                self.mm(acc[0][0:65, (n % 4) * 128:(n % 4) * 128 + 128], vap, pap, idx == 0, idx == len(parts) - 1, [Tpp, Tv], [acc[1]])
                            if n % 4 == 3:
                                emit(n - 3, acc[0], acc[1])
                    else:
                        r0 = m0
                        ab = ACCB[st["acc"] % 2]; st["acc"] += 1
                        for j in range(4):
                            self.mm(self.ps[ab][0:65, j * 128:(j + 1) * 128], Vd4[:, 32 + r0 + j, hh, :], cur[0][:, j * 128:(j + 1) * 128], True, True,
                                    [cur[1], Tv], [self.Tps[ab]])
                        uv = UT.rearrange("c (j r) -> c r j", r=16)[:, r0:r0 + 4, :]
                        self.tt("dve", uv, self.ps[ab][0:65, :].rearrange("c (r j) -> c r j", j=128), uv, ALU.add, [self.Tps[ab], TU], [TU])
                nu = len(units)
                for i in range(nu + LA):
                    if i < nu:
                        stage1(i)
                    if i - LA >= 0:
                        stage2(i - LA)
                for t4 in range(4):
                    tb = TRB[t4 % 2]
                    pt_ = self.ps[tb]; Tt_ = self.Tps[tb]
                    for j in range(4):
                        tile_ = t4 * 4 + j
                        self.tr(pt_[:, j * 65:(j + 1) * 65], UT[:, tile_ * 128:(tile_ + 1) * 128], self.cst[0:65, C_ID:C_ID + 65], [TU, self.Tcst], [Tt_])
                    self.recip(rz[:, 0, :], pt_[:, 0:260].rearrange("p (j c) -> p j c", c=65)[:, :, 64], [Tt_], [Trz])
                    for j in range(4):
                        tile_ = t4 * 4 + j
                        self.ts("dve", self.o_all[:, tile_, 1536 + h * 64:1536 + (h + 1) * 64], pt_[:, j * 65:j * 65 + 64], rz[:, 0, j:j + 1], ALU.mult,
                                [Tt_, Trz], [self.To[tile_][3]])
            s.barrier()


    def phase_final(self, b, l):
        s = self.s
        self.a_reset()
        src, Tsrc = self.xsrc(b, l)
        last = (l == self.layers[-1])
        dst = self.y[b] if last else self.xs[b]
        Tdst = self.Ty[b] if last else self.Txs[b]
        NB = 2
        mark = self.a_off
        zs = [self.a_get([1, 512], BF16) for _ in range(NB)]
        Tz = [T() for _ in range(NB)]
        for j in range(4):
            wv, Tw_ = self.wload(self.win_cols(l, OFF_Z + j * 512, 512), [KC, 512])
            for tt in range(NT):
                i = tt % NB
                ps = self.ps[tt % 2]; Tp = self.Tps[tt % 2]
                self.proj_tm(tt, wv, Tw_, 512, ps[:], Tp)
                self.act(zs[i][:, 0, :], ps[:], AF.Silu, [Tp], [Tz[i]])
                osl = self.o_all[:, tt, j * 512:(j + 1) * 512]
                self.tt("dve" if tt % 2 == 0 else "pool", osl, osl, zs[i][:, 0, :], ALU.mult, [Tz[i], self.To[tt][j]], [self.To[tt][j]])
        s.barrier()
        self.a_off = mark
        HT = NT // 2
        mg = self.a_get([HT, D_], F32)
        Tmg = [T() for _ in range(HT)]
        gbT = self.a_get([HT, 512], BF16)
        TgT = [T() for _ in range(HT)]
        sg = [self.a_get([1, 512], BF16) for _ in range(NB)]
        tmpf = [self.a_get([1, 512], F32) for _ in range(NB)]
        mgb = [self.a_get([1, D_], BF16)] * NB
        mT = [self.a_get([KC, 128], BF16)] * NB
        xt = [self.a_get([1, D_], F32) for _ in range(NB)]
        st = [self.a_get([1, 12], F32) for _ in range(NB)]
        mv = [self.a_get([1, 4], F32) for _ in range(NB)]
        Tsg = [T() for _ in range(NB)]; Ttf = [T() for _ in range(NB)]; Tmb = [T()] * NB
        TmT = [T()] * NB; Tx = [T() for _ in range(NB)]; Tm = [T() for _ in range(NB)]
        wo = [self.w_out[l].rearrange("(kc p) d -> p kc d", p=128)[:, :, hh * 512:(hh + 1) * 512] for hh in range(2)]
        for half in range(2):
            t0 = half * HT
            for n in range(4):
                for tl in range(HT):
                    tt = t0 + tl
                    tb = 2 + (tl % 2)
                    pv = self.ps[tb][:].bitcast(BF16); Tt_ = self.Tps[tb]
                    for c in range(4):
                        self.tr(pv[:, c * 128:(c + 1) * 128], self.o_all[:, tt, n * 512 + c * 128:n * 512 + (c + 1) * 128], self.identb[:],
                                [self.To[tt][n], self.Tcb], [Tt_])
                    self.cp("act", gbT[:, tl, :], pv[:, 0:512], [Tt_], [TgT[tl]])
                for hh in range(2):
                    wg, Twg = self.wload(self.win_cols(l, OFF_MG + n * 1024 + hh * 512, 512), [KC, 512])
                    wb, Twb = self.wload(self.w_br[l, n].rearrange("(kc p) d -> p kc d", p=128)[:, :, hh * 512:(hh + 1) * 512], [4, 512])
                    for tl in range(HT):
                        tt = t0 + tl
                        i = tl % NB
                        pg = self.ps[tl % 2]; Tpg = self.Tps[tl % 2]
                        self.proj_tm(tt, wg, Twg, 512, pg[:], Tpg)
                        self.act(sg[i][:, 0, :], pg[:], AF.Sigmoid, [Tpg], [Tsg[i]])
                        py = self.ps[4 + (tl % 2)]; Tpy = self.Tps[4 + (tl % 2)]
                        for kc in range(4):
                            self.mm(py[:], gbT[:, tl, kc * 128:(kc + 1) * 128], wb[:, kc, :], kc == 0, kc == 3, [TgT[tl], Twb], [Tpy])
                        msl = mg[:, tl, hh * 512:(hh + 1) * 512]
                        if n == 0:
                            self.tt("dve", msl, py[:], sg[i][:, 0, :], ALU.mult, [Tpy, Tsg[i]], [Tmg[tl]])
                        else:
                            self.tt("dve", tmpf[i][:, 0, :], py[:], sg[i][:, 0, :], ALU.mult, [Tpy, Tsg[i]], [Ttf[i]])
                            self.tt("dve", msl, msl, tmpf[i][:, 0, :], ALU.add, [Ttf[i], Tmg[tl]], [Tmg[tl]])
            wvs = [self.wload(wo[hh], [KC, 512]) for hh in range(2)]
            for tl in range(HT):
                tt = t0 + tl
                i = tl % NB
                self.cp("dve", mgb[i][:, 0, :], mg[:, tl, :], [Tmg[tl]], [Tmb[i]])
                tb = 2 + (tl % 2)
                pv = self.ps[tb][:].bitcast(BF16); Tt_ = self.Tps[tb]
                for c in range(KC):
                    self.tr(pv[:, c * 128:(c + 1) * 128], mgb[i][:, 0, c * 128:(c + 1) * 128], self.identb[:], [Tmb[i], self.Tcb], [Tt_])
                self.cp("act", mT[i][:, :, :], pv[:, 0:1024].rearrange("p (c t) -> p c t", t=128), [Tt_], [TmT[i]])
                R = [] if Tsrc is None else [Tsrc[tt]]
                s.dma("sp", xt[i][:, 0, :], src[tt * 128:(tt + 1) * 128, :], reads=R, writes=[Tx[i]])
                for hh in range(2):
                    wv, Tw_ = wvs[hh]
                    bank = 6 + hh
                    ps = self.ps[bank]; Tp = self.Tps[bank]
                    for kc in range(KC):
                        self.mm(ps[:], mT[i][:, kc, :], wv[:, kc, :], kc == 0, kc == KC - 1, [TmT[i], Tw_], [Tp])
                    xs_ = xt[i][:, 0, hh * 512:(hh + 1) * 512]
                    self.tt("dve", tmpf[i][:, 0, :], ps[:], self.gate_bc[:, hh * 512:(hh + 1) * 512], ALU.mult, [Tp, self.Tgate], [Ttf[i]])
                    self.stt("dve", xs_, xs_, ALPHA, tmpf[i][:, 0, :], ALU.mult, ALU.add, [Tx[i], Ttf[i]], [Tx[i]])
                    s.op("dve", (lambda o_, i_: (lambda e: e.bn_stats(out=o_, in_=i_)))(st[i][:, 0, hh * 6:(hh + 1) * 6], xs_), [Tx[i]], [Tm[i]])
                s.op("dve", (lambda o_, i_: (lambda e: e.bn_aggr(out=o_, in_=i_)))(mv[i][:, 0, 0:2], st[i][:, 0, :]), [Tm[i]], [Tm[i]])
                self.rsqrt(mv[i][:, 0, 2:3], mv[i][:, 0, 1:2], 1.0, [Tm[i]], [Tm[i]])
                self.stt("dve", mv[i][:, 0, 3:4], mv[i][:, 0, 0:1], -1.0, mv[i][:, 0, 2:3], ALU.mult, ALU.mult, [Tm[i]], [Tm[i]])
                self.act(xt[i][:, 0, :], xt[i][:, 0, :], AF.Identity, [Tx[i], Tm[i]], [Tx[i]], bias=mv[i][:, 0, 3:4], scale=mv[i][:, 0, 2:3])
                self.tt("dve", xt[i][:, 0, :], xt[i][:, 0, :], self.lng[:], ALU.mult, [Tx[i], self.Tlnp], [Tx[i]])
                self.tt("dve", xt[i][:, 0, :], xt[i][:, 0, :], self.lnb[:], ALU.add, [Tx[i], self.Tlnp], [Tx[i]])
                s.dma("sp", dst[tt * 128:(tt + 1) * 128, :], xt[i][:, 0, :], reads=[Tx[i]], writes=[Tdst[tt]])


_CACHE = {}


def host_inputs(inputs, core, nseq):
    f = lambda a: np.ascontiguousarray(np.asarray(a))
    b0 = core * nseq
    x = f(inputs["x"][b0:b0 + nseq])
    c = np.asarray(inputs["c"][b0:b0 + nseq])
    cT = f(c.T.reshape(KC, 128, nseq).transpose(1, 0, 2))
    pos = np.asarray(inputs["positions"][b0:b0 + nseq]).astype(np.int32)
    pos_tm = f(pos.reshape(nseq, NT, 128).transpose(0, 2, 1))
    w_in = np.asarray(inputs["w_in"])
    wd = w_in[:, :, OFF_D:OFF_D + 4608].reshape(DEPTH, D_, 3, 3, 4, 2, 64)
    wd = f(wd.transpose(0, 1, 4, 2, 3, 5, 6).reshape(DEPTH, D_, 4608))
    conv = np.asarray(inputs["conv_b"])
    conv_fm = f(conv.reshape(DEPTH, 4, 12, 128).transpose(0, 3, 2, 1))
    b_ada = np.asarray(inputs["b_ada"])
    b_ada_fm = f(b_ada.reshape(DEPTH, 24, 128).transpose(0, 2, 1))
    qn = np.asarray(inputs["q_norm_c"])
    kvn = np.asarray(inputs["kv_norm_c"])
    m = {
        "x": x, "pos_tm": pos_tm, "cT": cT,
        "w_ada": f(inputs["w_ada"]), "b_ada": f(b_ada), "b_ada_fm": b_ada_fm,
        "w_in": f(w_in), "w_d": wd, "conv_fm": conv_fm,
        "a_log": f(inputs["a_log"]), "dt_bias": f(inputs["dt_bias"]), "out_norm_b": f(inputs["out_norm_b"]),
        "lambda_q1": f(inputs["lambda_q1"]), "lambda_k1": f(inputs["lambda_k1"]),
        "lambda_q2": f(inputs["lambda_q2"]), "lambda_k2": f(inputs["lambda_k2"]),
        "subln_g": f(inputs["subln_g"]),
        "qn_fm": f(qn.reshape(DEPTH, 2, 128).transpose(0, 2, 1)), "kvn_fm": f(kvn.reshape(DEPTH, 1, 128).transpose(0, 2, 1)),
        "w_uq": f(inputs["w_uq"]), "w_ukv": f(inputs["w_ukv"]), "w_br": f(inputs["w_br"]), "w_out": f(inputs["w_out"]),
        "ln_g": f(inputs["ln_g"]), "ln_b": f(inputs["ln_b"]),
        "consts": make_consts(),
    }
    return m


def kernel(**inputs):
    n_cores = 8
    nseq = 4
    bld = Builder(nseq=nseq)
    shared = None
    in_maps = []
    for core in range(n_cores):
        m = host_inputs(inputs, core, nseq)
        if shared is None:
            shared = m
        else:
            for k in m:
                if k not in ("x", "pos_tm", "cT"):
                    m[k] = shared[k]
        in_maps.append(m)
    res = run_bass_kernel_spmd(bld.nc, in_maps, core_ids=list(range(n_cores)))
    return np.concatenate([np.asarray(r["y"]) for r in res.results], axis=0).astype(np.float32)
```
